# Optimizing a Trainium2 kernel written in Bass

```python
import math
import jax, jax.numpy as jnp
from jax import lax
import numpy as np

D_MODEL = 1024
BATCH = 32
SEQ = 2048
DEPTH = 2

HEAD_DIM = 64
GRID_W = 64
Q_BLOCK = 128
ROPE_THETA = 10000.0
EPS = 1e-6
FORGET_FLOOR = 1e-6

A_HEADS = 6
A_KV_HEADS = 2
B_HEADS = 6
B_QK_DIM = HEAD_DIM // 2
C_HEADS = 4
C_KEY_DIM = 64
C_VAL_DIM = 64
C_CHUNK = 64

A_WIDTH = A_HEADS * HEAD_DIM
B_WIDTH = B_HEADS * HEAD_DIM
C_WIDTH = C_HEADS * C_VAL_DIM
MIX_WIDTH = A_WIDTH + B_WIDTH + C_WIDTH

IN_SIZES = [A_HEADS * HEAD_DIM, A_KV_HEADS * HEAD_DIM, A_KV_HEADS * HEAD_DIM,
            B_HEADS * 2 * B_QK_DIM, B_HEADS * 2 * B_QK_DIM, B_HEADS * HEAD_DIM,
            C_HEADS * C_KEY_DIM, C_HEADS * C_KEY_DIM, C_HEADS * C_KEY_DIM,
            C_HEADS * C_VAL_DIM, C_HEADS * C_VAL_DIM]
IN_TOTAL = sum(IN_SIZES)
D_FF = 4 * D_MODEL

kernel_name = "hymba_style_bidir_hybrid_encoder"


def rms_norm(x, gain=None):
    xf = x.astype(jnp.float32)
    y = xf * lax.rsqrt(jnp.mean(xf * xf, axis=-1, keepdims=True) + EPS)
    if gain is not None:
        y = y * gain.astype(jnp.float32)
    return y.astype(x.dtype)


def rope_angles(pos, dim):
    inv = ROPE_THETA ** (-jnp.arange(0, dim, 2, dtype=jnp.float32) / dim)
    return pos.astype(jnp.float32)[:, None] * inv[None, :]


def apply_rope(x, ang):
    shape = (1, ang.shape[0]) + (1,) * (x.ndim - 3) + (ang.shape[1],)
    cos = jnp.cos(ang).reshape(shape)
    sin = jnp.sin(ang).reshape(shape)
    xf = x.astype(jnp.float32)
    x1, x2 = jnp.split(xf, 2, axis=-1)
    out = jnp.concatenate([x1 * cos - x2 * sin, x1 * sin + x2 * cos], axis=-1)
    return out.astype(x.dtype)


def axial_rope(x, ang_row, ang_col):
    half = x.shape[-1] // 2
    return jnp.concatenate([apply_rope(x[..., :half], ang_row), apply_rope(x[..., half:], ang_col)], axis=-1)


def blocked_attention(q, k, v):
    B, S, H, dq = q.shape
    Hkv = k.shape[2]
    G = H // Hkv
    dv = v.shape[-1]
    nb = S // Q_BLOCK
    scale = dq ** -0.5
    qb = q.reshape(B, nb, Q_BLOCK, Hkv, G, dq).transpose(1, 0, 2, 3, 4, 5)

    def one_block(qblk):
        s = jnp.einsum('bqkgd,bskd->bkgqs', qblk, k).astype(jnp.float32) * scale
        p = jax.nn.softmax(s, axis=-1).astype(v.dtype)
        return jnp.einsum('bkgqs,bskd->bqkgd', p, v)

    o = lax.map(one_block, qb)
    return o.transpose(1, 0, 2, 3, 4, 5).reshape(B, S, H, dv)


def hgrn2_chunk_scan(q, logf, k, v):
    B, S, H, dk = q.shape
    dv = v.shape[-1]
    n = S // C_CHUNK

    def to_chunks(a):
        return a.reshape(B, n, C_CHUNK, H, a.shape[-1]).transpose(1, 0, 3, 2, 4)

    mask = jnp.tril(jnp.ones((C_CHUNK, C_CHUNK), dtype=bool))[None, None, :, :, None]

    def step(state, inp):
        qi, gi, ki, vi = inp
        bcum = jnp.cumsum(gi, axis=2)
        diff = bcum[:, :, :, None, :] - bcum[:, :, None, :, :]
        decay = jnp.where(mask, jnp.exp(jnp.where(mask, diff, 0.0)), 0.0)
        scores = jnp.einsum('bhtd,bhtsd,bhsd->bhts', qi, decay, ki)
        o = jnp.einsum('bhts,bhse->bhte', scores, vi) + jnp.einsum('bhtd,bhde->bhte', qi * jnp.exp(bcum), state)
        blast = bcum[:, :, -1:, :]
        new_state = jnp.exp(blast[:, :, 0, :])[..., None] * state + jnp.einsum('bhsd,bhse->bhde', ki * jnp.exp(blast - bcum), vi)
        return new_state, o

    init = jnp.zeros((B, H, dk, dv), jnp.float32)
    _, o = lax.scan(step, init, (to_chunks(q), to_chunks(logf), to_chunks(k), to_chunks(v)))
    return o.transpose(1, 0, 3, 2, 4).reshape(B, S, H, dv)


def mixer_axial_gqa(zq, zk, zv, qk_gains, ang_row, ang_col):
    B, S, _ = zq.shape
    q = rms_norm(zq.reshape(B, S, A_HEADS, HEAD_DIM), qk_gains[0])
    k = rms_norm(zk.reshape(B, S, A_KV_HEADS, HEAD_DIM), qk_gains[1])
    v = zv.reshape(B, S, A_KV_HEADS, HEAD_DIM)
    q = axial_rope(q, ang_row, ang_col)
    k = axial_rope(k, ang_row, ang_col)
    return blocked_attention(q, k, v).reshape(B, S, A_WIDTH)


def mixer_diff_attention(zq, zk, zv, lam_params, subln_gain, lam_init, ang):
    B, S, _ = zq.shape
    q = apply_rope(zq.reshape(B, S, B_HEADS, 2, B_QK_DIM), ang)
    k = apply_rope(zk.reshape(B, S, B_HEADS, 2, B_QK_DIM), ang)
    v = zv.reshape(B, S, B_HEADS, HEAD_DIM)
    lp = lam_params.astype(jnp.float32)
    lam = jnp.exp(jnp.sum(lp[0] * lp[1])) - jnp.exp(jnp.sum(lp[2] * lp[3])) + lam_init
    o1 = blocked_attention(q[:, :, :, 0], k[:, :, :, 0], v)
    o2 = blocked_attention(q[:, :, :, 1], k[:, :, :, 1], v)
    o = o1 - lam.astype(o1.dtype) * o2
    o = rms_norm(o, subln_gain) * (1.0 - lam_init)
    return o.reshape(B, S, B_WIDTH)


def mixer_hgrn2(zq, zf_fwd, zf_bwd, zi, zg, lb, norm_gain):
    B, S, _ = zq.shape

    def heads(a, d):
        return a.reshape(B, S, C_HEADS, d).astype(jnp.float32)

    q = jax.nn.silu(heads(zq, C_KEY_DIM)) * (C_KEY_DIM ** -0.5)
    v = heads(zi, C_VAL_DIM)
    lb = lb.reshape(C_HEADS, C_KEY_DIM)

    def gates(zf):
        zf = heads(zf, C_KEY_DIM)
        f = lb + (1.0 - lb) * jax.nn.sigmoid(zf)
        logf = jnp.log(jnp.maximum(f, FORGET_FLOOR))
        key = (1.0 - lb) * jax.nn.sigmoid(-zf)
        return logf, key

    logf_f, k_f = gates(zf_fwd)
    logf_b, k_b = gates(zf_bwd)
    flip = lambda a: jnp.flip(a, axis=1)
    o_f = hgrn2_chunk_scan(q, logf_f, k_f, v)
    o_b = flip(hgrn2_chunk_scan(flip(q), flip(logf_b), flip(k_b), flip(v)))
    o = rms_norm(o_f + o_b, norm_gain) * jax.nn.silu(heads(zg, C_VAL_DIM))
    return o.reshape(B, S, C_WIDTH).astype(zq.dtype)


def setup_inputs(seed: int = 0) -> dict:
    key = jax.random.key(seed)
    ks = jax.random.split(key, 14)
    nrm = jax.random.normal
    f32 = jnp.float32
    return {
        "x": nrm(ks[0], (BATCH, SEQ, D_MODEL), f32),
        "c": nrm(ks[1], (BATCH, D_MODEL), f32),
        "w_mod": nrm(ks[2], (DEPTH, D_MODEL, 6 * D_MODEL), f32) * (0.5 * D_MODEL ** -0.5),
        "b_mod": nrm(ks[3], (DEPTH, 6 * D_MODEL), f32) * 0.02,
        "w_in": nrm(ks[4], (DEPTH, D_MODEL, IN_TOTAL), f32) * (D_MODEL ** -0.5),
        "a_qk_norm": 1.0 + 0.02 * nrm(ks[5], (DEPTH, 2, HEAD_DIM), f32),
        "diff_lambda": 0.1 * nrm(ks[6], (DEPTH, 4, B_QK_DIM), f32),
        "diff_subln": 1.0 + 0.02 * nrm(ks[7], (DEPTH, HEAD_DIM), f32),
        "hgrn_lower_bounds": 0.1 * nrm(ks[8], (DEPTH, C_HEADS * C_KEY_DIM), f32),
        "hgrn_norm": 1.0 + 0.02 * nrm(ks[9], (DEPTH, C_VAL_DIM), f32),
        "w_out": nrm(ks[10], (DEPTH, MIX_WIDTH, D_MODEL), f32) * (MIX_WIDTH ** -0.5),
        "w_ff1": nrm(ks[11], (DEPTH, D_MODEL, D_FF), f32) * (D_MODEL ** -0.5),
        "w_ff2": nrm(ks[12], (DEPTH, D_FF, D_MODEL), f32) * (D_FF ** -0.5),
        "final_norm": 1.0 + 0.02 * nrm(ks[13], (D_MODEL,), f32),
    }


def reference(x, c, w_mod, b_mod, w_in, a_qk_norm, diff_lambda, diff_subln, hgrn_lower_bounds, hgrn_norm, w_out, w_ff1, w_ff2, final_norm):
    S = x.shape[1]
    rows = S // GRID_W
    t = jnp.arange(S)
    row = jnp.repeat(jnp.arange(rows), GRID_W)
    col = jnp.tile(jnp.arange(GRID_W), rows)
    ang_row = rope_angles(row, HEAD_DIM // 2)
    ang_col = rope_angles(col, HEAD_DIM // 2)
    ang_1d = rope_angles(t, B_QK_DIM)

    lbp = jax.nn.softmax(hgrn_lower_bounds.astype(jnp.float32), axis=0)
    lbs = jnp.clip(jnp.cumsum(lbp, axis=0) - lbp[0:1], 0.0, 1.0)

    split_idx = [int(i) for i in np.cumsum(IN_SIZES)[:-1]]
    cond = jax.nn.silu(c)
    for l in range(DEPTH):
        mod = cond @ w_mod[l] + b_mod[l]
        sh1, sc1, g1, sh2, sc2, g2 = [m[:, None, :] for m in jnp.split(mod, 6, axis=-1)]

        h = rms_norm(x) * (1.0 + sc1) + sh1
        z = h @ w_in[l]
        aq, ak, av, bq, bk, bv, cq, cff, cfb, ci, cg = jnp.split(z, split_idx, axis=-1)
        o_a = mixer_axial_gqa(aq, ak, av, a_qk_norm[l], ang_row, ang_col)
        lam_init = 0.8 - 0.6 * math.exp(-0.3 * l)
        o_b = mixer_diff_attention(bq, bk, bv, diff_lambda[l], diff_subln[l], lam_init, ang_1d)
        o_c = mixer_hgrn2(cq, cff, cfb, ci, cg, lbs[l], hgrn_norm[l])
        x = x + g1 * (jnp.concatenate([o_a, o_b, o_c], axis=-1) @ w_out[l])

        h = rms_norm(x) * (1.0 + sc2) + sh2
        x = x + g2 * (jnp.square(jax.nn.relu(h @ w_ff1[l])) @ w_ff2[l])
    return rms_norm(x, final_norm)
```

```python
import math
import numpy as np
from contextlib import ExitStack
import concourse.bass as bass
import concourse.mybir as mybir
from concourse.bass_utils import run_bass_kernel_spmd

F32 = mybir.dt.float32
BF16 = mybir.dt.bfloat16
ALU = mybir.AluOpType
AF = mybir.ActivationFunctionType

NCORES = 8
S = 2048
D = 1024
NT = 16
NB = 4
KC = 8
EPS = 1e-6
LN_FLOOR = math.log(1e-6)
ENGS = ["pe", "act", "dve", "pool", "sp"]


class _Stop(Exception):
    pass


class Op:
    __slots__ = ("eng", "fn", "deps", "pos", "dma", "dma_cnt", "waits", "signals", "sig_idx", "snap")


class Prog:
    def __init__(self, same_engine_sync=True):
        self.ops = {e: [] for e in ENGS}
        self.order = []
        self.lastw = {}
        self.readers = {}
        self.dma_count = {}
        self.same_engine_sync = same_engine_sync
        self.bar = []
        self.dma_since_bar = []

    def op(self, eng, fn, reads=(), writes=(), dma=None):
        o = Op()
        o.eng = eng
        o.fn = fn
        o.dma = dma
        o.signals = False
        o.sig_idx = 0
        o.waits = []
        o.snap = None
        deps = []
        seen = set()

        def add(d):
            if d is None or id(d) in seen:
                return
            seen.add(id(d))
            if d.dma is None and d.eng == eng:
                if eng in ("pe", "sp"):
                    return
                if not self.same_engine_sync:
                    return
            deps.append(d)

        for d in self.bar:
            add(d)
        for k in reads:
            add(self.lastw.get(k))
        for k in writes:
            add(self.lastw.get(k))
            for r in self.readers.get(k, ()):
                add(r)
        o.deps = deps
        for k in reads:
            self.readers.setdefault(k, []).append(o)
        for k in writes:
            self.lastw[k] = o
            self.readers[k] = []
        o.pos = len(self.ops[eng])
        self.ops[eng].append(o)
        self.order.append(o)
        if dma is not None:
            c = self.dma_count.get(dma, 0) + 1
            self.dma_count[dma] = c
            o.dma_cnt = c
            self.dma_since_bar.append(o)
        else:
            o.dma_cnt = 0
        return o

    def barrier(self):
        bar = []
        for e in ENGS:
            for o in reversed(self.ops[e]):
                if o.dma is None and o.fn is not None:
                    bar.append(o)
                    break
        last_dma = {}
        for o in self.dma_since_bar:
            last_dma[o.dma] = o
        for o in self.bar:
            if o.dma is not None and o.dma not in last_dma:
                last_dma[o.dma] = o
        bar.extend(last_dma.values())
        self.bar = bar
        self.dma_since_bar = []

    def resolve(self):
        known = {e: {} for e in ENGS}
        for o in self.order:
            kn = known[o.eng]
            need = {}
            for d in o.deps:
                if d.dma is not None:
                    key = ("dma", d.dma)
                    val = d.dma_cnt
                else:
                    key = ("eng", d.eng)
                    val = d.pos + 1
                if key not in need or need[key][0] < val:
                    need[key] = (val, d)
            for key, (val, d) in need.items():
                if kn.get(key, 0) >= val:
                    continue
                o.waits.append(d)
                d.signals = True
                kn[key] = val
                if d.snap is not None:
                    for k2, v2 in d.snap.items():
                        if kn.get(k2, 0) < v2:
                            kn[k2] = v2
            o.snap = dict(kn)
        for e in ENGS:
            c = 0
            for o in self.ops[e]:
                if o.dma is None and o.signals:
                    c += 1
                    o.sig_idx = c
        for o in self.order:
            o.snap = None

    def emit(self, nc, stack):
        self.resolve()
        sems = {}
        for e in ENGS:
            sems[("eng", e)] = stack.enter_context(nc.semaphore("s_" + e))
        for name in self.dma_count:
            sems[("dma", name)] = stack.enter_context(nc.semaphore("d_" + str(name)))
        block = stack.enter_context(nc.Block())
        hook = {"pe": block.tensor, "act": block.scalar, "dve": block.vector, "pool": block.gpsimd, "sp": block.sync}

        def make(e):
            def body(eng):
                for o in self.ops[e]:
                    for d in o.waits:
                        if d.dma is not None:
                            eng.wait_ge(sems[("dma", d.dma)], 16 * d.dma_cnt)
                        else:
                            eng.wait_ge(sems[("eng", d.eng)], d.sig_idx)
                    if o.fn is None:
                        continue
                    inst = o.fn(eng)
                    if o.dma is not None:
                        inst.then_inc(sems[("dma", o.dma)], 16)
                    elif o.signals:
                        inst.then_inc(sems[("eng", e)], 1)
            return body

        for e in ENGS:
            if self.ops[e]:
                hook[e](make(e))


def _host_consts():
    p = np.arange(128)
    ones64 = (p[:, None] // 64 == p[None, :] // 64).astype(np.float32)
    partner = np.where(p % 32 < 16, p + 16, p - 16)
    perm = np.zeros((128, 128), np.float32)
    perm[partner, p] = 1.0
    ident = np.eye(128, dtype=np.float32)
    same = (p[:, None] // 32 == p[None, :] // 32)
    mf = (same & (p[:, None] <= p[None, :])).astype(np.float32)
    mb = (same & (p[:, None] >= p[None, :])).astype(np.float32)
    ones = np.ones((128, 128), np.float32)
    mats = np.stack([ones, ones64, perm, ident, mf, mf, mb, mb], axis=1)
    bmcol = np.zeros((128, 8), np.float32)
    bmcol[:, 0:4] = (p[:, None] // 32 == np.arange(4)[None, :])
    bmcol[:, 4] = (p % 64 < 32)
    bmcol[:, 5] = (p % 64 >= 32)
    inv = (np.float32(10000.0) ** (-np.arange(0, 32, 2, dtype=np.float32) / np.float32(32))).astype(np.float32)
    t = np.arange(S)
    row = (t // 64).astype(np.float32)
    col = (t % 64).astype(np.float32)
    tt = t.astype(np.float32)
    j = p % 64
    i16 = (p % 32) % 16
    sign = np.where(p % 32 < 16, -1.0, 1.0).astype(np.float32)
    angA = np.where((j < 32)[:, None], row[None, :] * inv[i16][:, None], col[None, :] * inv[i16][:, None]).astype(np.float32)
    angB = (tt[None, :] * inv[i16][:, None]).astype(np.float32)
    rope = np.stack([np.cos(angA), sign[:, None] * np.sin(angA), np.cos(angB), sign[:, None] * np.sin(angB)], axis=0).astype(np.float32)
    return mats.astype(np.float32), bmcol, rope


AQ, AK, AV, BQ, BK, BV, CQ, CFF, CFB, CI, CG = 0, 384, 512, 640, 1024, 1408, 1792, 2048, 2304, 2560, 2816
IN_PIECES = [("A", [(AQ, 384), (AK, 64), (AK, 64), (AK + 64, 64), (AK + 64, 64), (AV, 128)])]
for _j in range(3):
    IN_PIECES.append(("B%d" % _j, [(BQ + 128 * _j, 128), (BK + 128 * _j, 128), (BV + 128 * _j, 128)]))
for _j in range(2):
    IN_PIECES.append(("C%d" % _j, [(CQ + 128 * _j, 128), (CFF + 128 * _j, 128), (CFB + 128 * _j, 128), (CG + 128 * _j, 128), (CI + 128 * _j, 128)]))
IN_COLS = {n: sum(c for _, c in segs) for n, segs in IN_PIECES}


def build(nseq=4, depth=2, debug=False, same_engine_sync=True, stop=None):
    nc = bass.Bass("TRN2", target_bir_lowering=False)
    P = Prog(same_engine_sync=same_engine_sync)

    def din(name, shape, dt=F32):
        return nc.dram_tensor(name, list(shape), dt, kind="ExternalInput").ap()

    x_d = din("x", [nseq, S, D])
    cT_d = din("cT", [128, 8 * nseq])
    wmod_d = din("w_mod", [2, D, 6 * D])
    bmod_d = din("b_modT", [128, 96])
    win_d = din("w_in", [2, D, 3072])
    wout_d = din("w_out", [2, D, D])
    wff1_d = din("w_ff1", [2, D, 4 * D])
    wff2_d = din("w_ff2", [2, 4 * D, D])
    vecs_d = din("vecs", [128, 32])
    dlam_d = din("dlam", [128, 256])
    rope_d = din("rope", [4, 128, S])
    mats_d = din("mats", [128, 8 * 128])
    bmcol_d = din("bmcol", [128, 8])
    out_d = nc.dram_tensor("out", [nseq, S, D], F32, kind="ExternalOutput").ap()
    dbg = {}
    if debug:
        for nm in ("dbg_ct", "dbg_xa", "dbg_h"):
            dbg[nm] = nc.dram_tensor(nm, [128, 8 * S], F32, kind="ExternalOutput").ap()
        dbg["dbg_mod"] = nc.dram_tensor("dbg_mod", [128, 2 * 48 * nseq], F32, kind="ExternalOutput").ap()
        dbg["dbg_dv"] = nc.dram_tensor("dbg_dv", [128, 64], F32, kind="ExternalOutput").ap()

    scr = {}
    for l in range(depth):
        for n, _ in IN_PIECES:
            scr[("in", l, n)] = nc.dram_tensor("s_in_%d_%s" % (l, n), [128, 8 * IN_COLS[n]], BF16).ap()
        for h in range(2):
            scr[("out", l, h)] = nc.dram_tensor("s_out_%d_%d" % (l, h), [128, 8 * 512], BF16).ap()
        for j in range(8):
            scr[("ff1", l, j)] = nc.dram_tensor("s_ff1_%d_%d" % (l, j), [128, 8 * 512], BF16).ap()
            scr[("ff2", l, j)] = nc.dram_tensor("s_ff2_%d_%d" % (l, j), [128, 32 * 128], BF16).ap()

    with ExitStack() as st:
        ARENA_W = 53200
        arena = st.enter_context(nc.sbuf_tensor("arena", [128, ARENA_W], F32))
        psS = st.enter_context(nc.psum_tensor("psS", [128, 2048], F32))
        psb = [psS[:, i * 512:(i + 1) * 512] for i in range(4)]
        psb += [st.enter_context(nc.psum_tensor("ps%d" % i, [128, 512], F32))[:, :] for i in range(4, 7)]
        psT = st.enter_context(nc.psum_tensor("psT", [128, 1024], BF16))

        def V(off, words, dt=F32, pat=None, **kw):
            a = arena[:, off:off + words]
            if dt is not F32:
                a = a.bitcast(dt)
            if pat is not None:
                a = a.rearrange(pat, **kw)
            return a

        XT_O, HT_O, CT_O, CONST_O, WORK_O = 0, 16384, 24576, 32768, 34816
        WORK_W = ARENA_W - WORK_O
        xT = V(XT_O, 16384, F32, "p (k t) -> p k t", k=8)
        hT = V(HT_O, 8192, BF16, "p (k t) -> p k t", k=8)
        cT = V(CT_O, 8192, BF16, "p (k t) -> p k t", k=8)
        aT = V(CT_O, 8192, BF16, "p (k t) -> p k t", k=32)
        co = [CONST_O]

        def calloc(words):
            o = co[0]
            co[0] += words
            assert co[0] <= WORK_O
            return o

        ident_f = V(calloc(128), 128)
        matsb = V(calloc(8 * 64), 8 * 64, BF16, "p (m c) -> p m c", m=8)
        ones_b, ones64_b, perm_b, ident_b = matsb[:, 0, :], matsb[:, 1, :], matsb[:, 2, :], matsb[:, 3, :]
        maskb = V(CONST_O + 128 + 4 * 64, 4 * 64, BF16)
        vecs = V(calloc(32), 32)
        dv = V(calloc(64), 64)
        modT = V(calloc(2 * 48 * nseq), 2 * 48 * nseq, F32, "p (l j s) -> p l j s", l=2, j=48)
        condb = V(calloc(4 * nseq), 4 * nseq, BF16, "p (k s) -> p k s", k=8)
        bmcol = V(calloc(8), 8)
        rmask = V(calloc(512), 512)

        wk = [0]

        def walloc(words, dt=F32, pat=None, **kw):
            o = WORK_O + wk[0]
            wk[0] += words
            assert wk[0] <= WORK_W, ("work overflow", wk[0], WORK_W)
            return V(o, words, dt, pat, **kw)

        def wreset():
            P.barrier()
            wk[0] = 0

        rot = {"s": 0, "a": 0}

        def ps_s():
            i = rot["s"]
            rot["s"] = (i + 1) % 4
            return i

        def ps_s2():
            i = rot["s"]
            if i % 2:
                i = (i + 1) % 4
            rot["s"] = (i + 2) % 4
            return i

        def ps_a():
            i = 4 + rot["a"]
            rot["a"] = (rot["a"] + 1) % 3
            return i

        uid = [0]

        def K(name):
            uid[0] += 1
            return (name, uid[0])

        def mm(out, lhsT, rhs, start, stop, reads, writes, **kw):
            P.op("pe", lambda e: e.matmul(out, lhsT=lhsT, rhs=rhs, start=start, stop=stop, **kw), reads, writes)

        def tr(out, in_, ident, reads, writes):
            P.op("pe", lambda e: e.transpose(out, in_=in_, identity=ident), reads, writes)

        def act(out, in_, func, reads, writes, scale=1.0, bias=0.0):
            if func is AF.Copy:
                P.op("act", lambda e: e.activation(out=out, in_=in_, func=func, scale=scale), reads, writes)
            else:
                P.op("act", lambda e: e.activation(out=out, in_=in_, func=func, scale=scale, bias=bias), reads, writes)

        def tt(eng, out, in0, in1, op, reads, writes):
            P.op(eng, lambda e: e.tensor_tensor(out=out, in0=in0, in1=in1, op=op), reads, writes)

        def ts(eng, out, in0, s1, s2, op0, op1, reads, writes):
            if op1 is None:
                P.op(eng, lambda e: e.tensor_scalar(out=out, in0=in0, scalar1=s1, scalar2=None, op0=op0), reads, writes)
            else:
                P.op(eng, lambda e: e.tensor_scalar(out=out, in0=in0, scalar1=s1, scalar2=s2, op0=op0, op1=op1), reads, writes)

        def stt(out, in0, scalar, in1, op0, op1, reads, writes):
            P.op("dve", lambda e: e.scalar_tensor_tensor(out=out, in0=in0, scalar=scalar, in1=in1, op0=op0, op1=op1), reads, writes)

        def recip(out, in_, reads, writes):
            P.op("dve", lambda e: e.reciprocal(out=out, in_=in_), reads, writes)

        def copy(eng, out, in_, reads, writes):
            if eng == "act":
                act(out, in_, AF.Copy, reads, writes)
            else:
                P.op(eng, lambda e: e.tensor_copy(out=out, in_=in_), reads, writes)

        def dma(eng, out, in_, reads, writes, sem):
            P.op(eng, lambda e: e.dma_start(out=out, in_=in_), reads, writes, dma=sem)

        def memset(eng, ap, val, writes):
            P.op(eng, lambda e: e.memset(ap, val), (), writes)

        dma("sp", ident_f, mats_d[:, 3 * 128:4 * 128], (), ["ident_f"], "c0")
        dma("pool", V(CONST_O + 128, 8 * 64, BF16), mats_d, (), ["mats"], "c1")
        dma("sp", vecs, vecs_d, (), ["vecs"], "c0")
        dma("sp", bmcol, bmcol_d, (), ["bmcol"], "c0")
        memset("pool", rmask, 1.0, ["rmask"])
        memset("pool", dv, 0.0, ["dv"])
        memset("pool", V(CONST_O + 128 + 512 + 32 + 64, 2 * 48 * nseq), 0.0, ["modT"])
        P.op("pool", lambda e: e.memset(rmask.rearrange("p (c t) -> p c t", t=32)[:, :, 0:1], 0.0), ["rmask"], ["rmask"])

        SU = 20000
        dl = V(SU, 256, F32, "p (l a j) -> p l a j", l=2, a=4)
        bmodT = V(SU + 256, 96)
        cond_f = V(SU + 352, 8 * nseq)
        small = V(SU + 352 + 8 * nseq, 64)
        dma("sp", V(SU, 256), dlam_d, (), ["dl"], "c0")
        dma("sp", bmodT, bmod_d, (), ["bmodT"], "c0")
        dma("sp", cond_f, cT_d, (), ["cond_f"], "c0")
        P.barrier()
        act(V(CONST_O + 128 + 512 + 32 + 64 + 2 * 48 * nseq, 4 * nseq, BF16), cond_f, AF.Silu, ["cond_f"], ["condb"])

        for l in range(2):
            lam_init = 0.8 - 0.6 * math.exp(-0.3 * l)
            ts("dve", dv[:, l:l + 1], vecs[:, l:l + 1], 8.0, None, ALU.mult, None, ["vecs"], ["dv"])
            ts("dve", dv[:, 2 + l:3 + l], vecs[:, 2 + l:3 + l], 8.0, None, ALU.mult, None, ["vecs"], ["dv"])
            ts("dve", dv[:, 4 + l:5 + l], vecs[:, 4 + l:5 + l], 8.0 * (1.0 - lam_init), None, ALU.mult, None, ["vecs"], ["dv"])
            ts("dve", dv[:, 6 + l:7 + l], vecs[:, 6 + l:7 + l], 8.0, None, ALU.mult, None, ["vecs"], ["dv"])
            tt("dve", small[:, 0:32], dl[:, l, 0, :], dl[:, l, 1, :], ALU.mult, ["dl"], ["small"])
            P.op("dve", lambda e: e.reduce_sum(out=small[:, 32:33], in_=small[:, 0:32], axis=mybir.AxisListType.X), ["small"], ["small"])
            tt("dve", small[:, 0:32], dl[:, l, 2, :], dl[:, l, 3, :], ALU.mult, ["dl", "small"], ["small"])
            P.op("dve", lambda e: e.reduce_sum(out=small[:, 33:34], in_=small[:, 0:32], axis=mybir.AxisListType.X), ["small"], ["small"])
            act(small[:, 34:36], small[:, 32:34], AF.Exp, ["small"], ["small"])
            ts("dve", dv[:, 8 + l:9 + l], small[:, 35:36], small[:, 34:35], -lam_init, ALU.subtract, ALU.add, ["small"], ["dv"])
        h0, h1 = vecs[:, 16:18], vecs[:, 18:20]
        sm_m, sm_e0, sm_e1, sm_s, sm_p0, sm_p1 = [small[:, 40 + 2 * i:42 + 2 * i] for i in range(6)]
        tt("dve", sm_m, h0, h1, ALU.max, ["vecs", "small"], ["small"])
        tt("dve", sm_e0, h0, sm_m, ALU.subtract, ["small"], ["small"])
        tt("dve", sm_e1, h1, sm_m, ALU.subtract, ["small"], ["small"])
        act(sm_e0, sm_e0, AF.Exp, ["small"], ["small"])
        act(sm_e1, sm_e1, AF.Exp, ["small"], ["small"])
        tt("dve", sm_s, sm_e0, sm_e1, ALU.add, ["small"], ["small"])
        recip(sm_s, sm_s, ["small"], ["small"])
        tt("dve", sm_p0, sm_e0, sm_s, ALU.mult, ["small"], ["small"])
        tt("dve", sm_p1, sm_e1, sm_s, ALU.mult, ["small"], ["small"])
        lbv = small[:, 52:56]
        tt("dve", lbv[:, 0:2], sm_p0, sm_p0, ALU.subtract, ["small"], ["small"])
        tt("dve", lbv[:, 2:4], sm_p0, sm_p1, ALU.add, ["small"], ["small"])
        tt("dve", lbv[:, 2:4], lbv[:, 2:4], sm_p0, ALU.subtract, ["small"], ["small"])
        ts("dve", lbv, lbv, 0.0, 1.0, ALU.max, ALU.min, ["small"], ["small"])
        for i in range(4):
            b = 16 + 3 * i
            copy("dve", dv[:, b:b + 1], lbv[:, i:i + 1], ["small"], ["dv"])
            ts("dve", dv[:, b + 1:b + 2], lbv[:, i:i + 1], -1.0, 1.0, ALU.mult, ALU.add, ["small"], ["dv"])
            ts("dve", dv[:, b + 2:b + 3], lbv[:, i:i + 1], -1.0, None, ALU.add, None, ["small"], ["dv"])

        MSTG = 0
        for l in range(depth):
            wv = wmod_d[l].rearrange("(k p) c -> p k c", p=128)
            for g in range(8):
                stg = V(MSTG + (g % 2) * 6144, 6144, F32, "p (k c) -> p k c", k=8)
                wb = V(MSTG + 12288 + (g % 2) * 3072, 3072, BF16, "p (k c) -> p k c", k=8)
                ks, kb = ("mstg", g % 2), ("mwb", g % 2)
                dma("sp", stg, wv[:, :, g * 768:(g + 1) * 768], (), [ks], "mod%d" % (g % 2))
                copy(["dve", "pool"][g % 2], wb, stg, [ks], [kb])
                bank = ps_s()
                for jj in range(6):
                    for k in range(8):
                        mm(psb[bank][:, jj * nseq:(jj + 1) * nseq], wb[:, k, jj * 128:(jj + 1) * 128], condb[:, k, :], k == 0, k == 7,
                           [kb, "condb"], [("ps", bank)])
                for jj in range(6):
                    j = g * 6 + jj
                    add1 = 1.0 if (8 <= j < 16 or 32 <= j < 40) else 0.0
                    ts("dve", modT[:, l, j, :], psb[bank][:, jj * nseq:(jj + 1) * nseq], bmodT[:, l * 48 + j:l * 48 + j + 1], add1,
                       ALU.add, ALU.add, [("ps", bank), "bmodT"], ["modT"])

        P.barrier()
        CS = 0
        cast_i = [0]

        def cast_piece(dst, segs, kc, cols):
            i = cast_i[0]
            cast_i[0] += 1
            b = i % 2
            stg = V(CS + b * 6144, kc * cols, F32, "p (k c) -> p k c", k=kc)
            wb = V(CS + 12288 + b * 3072, (kc * cols) // 2, BF16)
            ks, kb = ("cstg", b), ("cwb", b)
            o = 0
            for src, c in segs:
                dma("sp", stg[:, :, o:o + c], src, (), [ks], "cl%d" % b)
                o += c
            eng = ["dve", "pool", "act"][i % 3]
            copy(eng, wb, V(CS + b * 6144, kc * cols, F32), [ks], [kb])
            dma("sp", dst, wb, [kb], [("scr", id(dst))], "cs%d" % b)
            return ("scr", id(dst))

        scr_key = {}
        for l in range(depth):
            wv = win_d[l].rearrange("(k p) c -> p k c", p=128)
            for n, segs in IN_PIECES:
                scr_key[("in", l, n)] = cast_piece(scr[("in", l, n)], [(wv[:, :, c0:c0 + c], c) for c0, c in segs], 8, IN_COLS[n])
            wv = wout_d[l].rearrange("(k p) c -> p k c", p=128)
            for h in range(2):
                scr_key[("out", l, h)] = cast_piece(scr[("out", l, h)], [(wv[:, :, h * 512:(h + 1) * 512], 512)], 8, 512)
            wv = wff1_d[l].rearrange("(k p) c -> p k c", p=128)
            for j in range(8):
                scr_key[("ff1", l, j)] = cast_piece(scr[("ff1", l, j)], [(wv[:, :, j * 512:(j + 1) * 512], 512)], 8, 512)
            wv = wff2_d[l].rearrange("(k p) c -> p k c", p=128)
            for g in range(8):
                i = cast_i[0]
                cast_i[0] += 1
                b = i % 2
                stg = V(CS + b * 6144, 4096, F32, "p (k c) -> p k c", k=4)
                wb = V(CS + 12288 + b * 3072, 2048, BF16, "p (k c) -> p k c", k=4)
                ks, kb = ("cstg", b), ("cwb", b)
                dma("sp", stg, wv[:, g * 4:(g + 1) * 4, :], (), [ks], "cl%d" % b)
                copy(["dve", "pool", "act"][i % 3], V(CS + 12288 + b * 3072, 2048, BF16), V(CS + b * 6144, 4096, F32), [ks], [kb])
                for j in range(8):
                    dst = scr[("ff2", l, j)].rearrange("p (k c) -> p k c", k=32)[:, g * 4:(g + 1) * 4, :]
                    dma("sp", dst, wb[:, :, j * 128:(j + 1) * 128], [kb], [K("scrff2")], "cs%d" % b)
                    scr_key[("ff2", l, j)] = ("scr_ff2_unused", l, j)

        if debug:
            dma("sp", dbg["dbg_mod"], V(CONST_O + 128 + 512 + 32 + 64, 2 * 48 * nseq), ["modT"], ["dbgo"], "dbg")
            dma("sp", dbg["dbg_dv"], dv, ["dv"], ["dbgo"], "dbg")

        def rstd_from_psum(bank, scale, bias, r_ap, rk):
            act(r_ap, psb[bank][:], AF.Sqrt, [("ps", bank)], [rk], scale=scale, bias=bias)
            recip(r_ap, r_ap, [rk], [rk])

        def norm_to(s, get_scale, get_bias, dst_fn, dst_keys_fn, tmp):
            sqb, rr, tmpf = tmp
            for tb in range(NB):
                tsl = slice(tb * 512, (tb + 1) * 512)
                bank = ps_s()
                for k in range(KC):
                    sk = ("sq", k % 2)
                    if k % 2 == 0:
                        act(sqb[k % 2], xT[:, k, tsl], AF.Square, ["xT"], [sk])
                    else:
                        tt("pool", sqb[k % 2], xT[:, k, tsl], xT[:, k, tsl], ALU.mult, ["xT"], [sk])
                    mm(psb[bank][:], ones_b, sqb[k % 2], k == 0, k == KC - 1, [sk, "mats"], [("ps", bank)])
                rk = ("nr", tb % 2)
                r = rr[tb % 2]
                rstd_from_psum(bank, 1.0 / D, EPS, r, rk)
                for k in range(KC):
                    tk = ("ntmp", k % 2)
                    tt("dve", tmpf[k % 2], xT[:, k, tsl], r, ALU.mult, ["xT", rk], [tk])
                    bias = get_bias(k)
                    out_ap = dst_fn(k, tb)
                    if bias is None:
                        act(out_ap, tmpf[k % 2], AF.Copy, [tk, "modT", "vecs"], dst_keys_fn(k, tb), scale=get_scale(k))
                    else:
                        act(out_ap, tmpf[k % 2], AF.Identity, [tk, "modT", "vecs"], dst_keys_fn(k, tb), scale=get_scale(k), bias=bias)

        def norm_tmp():
            sqb = [walloc(256, BF16) for _ in range(2)]
            rr = [walloc(512) for _ in range(2)]
            tmpf = [walloc(512) for _ in range(2)]
            return sqb, rr, tmpf

        def load_w(dst, key_scr, name, sem):
            dma("sp", dst, scr[key_scr], [scr_key[key_scr]], [name], sem)

        def zproj(bank, w, c0, tb, wkey):
            tsl = slice(tb * 512, (tb + 1) * 512)
            for k in range(KC):
                mm(psb[bank][:], w[:, k, c0:c0 + 128], hT[:, k, tsl], k == 0, k == KC - 1, [wkey, "hT"], [("ps", bank)])

        def vproj(bank, w, c0, ncols, tt_, wkey, col_off=0):
            tsl = slice(tt_ * 128, (tt_ + 1) * 128)
            for k in range(KC):
                mm(psb[bank][:, col_off:col_off + ncols], hT[:, k, tsl], w[:, k, c0:c0 + ncols], k == 0, k == KC - 1, [wkey, "hT"], [("ps", bank)])

        def rope_tile(bank_z, dst, dkey, tb, gain, do_norm, cosT, sinT, tmps, tag, dst2=None):
            a_b, sq_b, t1, t2, r = tmps
            tsl = slice(tb * 512, (tb + 1) * 512)
            ka, ksq, kt1, kt2, kr = [(tag, n) for n in ("a", "sq", "t1", "t2", "r")]
            if gain is None:
                act(a_b, psb[bank_z][:], AF.Copy, [("ps", bank_z)], [ka])
            else:
                act(a_b, psb[bank_z][:], AF.Copy, [("ps", bank_z), "dv"], [ka], scale=gain)
            bank_r = ps_s()
            mm(psb[bank_r][:], perm_b, a_b, True, True, [ka, "mats"], [("ps", bank_r)])
            if do_norm:
                act(sq_b, psb[bank_z][:], AF.Square, [("ps", bank_z)], [ksq])
                bank_n = ps_s()
                mm(psb[bank_n][:], ones64_b, sq_b, True, True, [ksq, "mats"], [("ps", bank_n)])
                rstd_from_psum(bank_n, 1.0, 64.0 * EPS, r, kr)
            tt("dve", t1, a_b, cosT[:, tsl], ALU.mult, [ka, "rope"], [kt1])
            tt("dve", t2, psb[bank_r][:], sinT[:, tsl], ALU.mult, [("ps", bank_r), "rope"], [kt2])
            if do_norm:
                tt("pool", t1, t1, t2, ALU.add, [kt1, kt2], [kt1])
                tt("dve", dst, t1, r, ALU.mult, [kt1, kr], [dkey])
            elif dst2 is None:
                tt("pool", dst, t1, t2, ALU.add, [kt1, kt2], [dkey])
            else:
                tt("pool", t1, t1, t2, ALU.add, [kt1, kt2], [kt1])
                ts("dve", dst, t1, bmcol[:, 4:5], None, ALU.mult, None, [kt1, "bmcol"], [dkey])
                ts("pool", dst2, t1, bmcol[:, 5:6], 1.0, ALU.mult, ALU.mult, [kt1, "bmcol"], [dkey])

        def run_pipeline(items, group=2, lag=1):
            n = len(items)
            assert n % group == 0
            ng = n // group
            AHEAD = 20
            pre_at = {}
            for i, it in enumerate(items):
                if it[0] is not None:
                    pre_at.setdefault(max(0, i // group - AHEAD), []).append(it[0])
            for step in range(ng + lag):
                for f in pre_at.get(step, ()):
                    f()
                if step < ng:
                    for it in items[step * group:(step + 1) * group]:
                        it[1]()
                    for it in items[step * group:(step + 1) * group]:
                        it[4]()
                j = step - lag
                if j >= 0:
                    for it in items[j * group:(j + 1) * group]:
                        it[2]()
                    for it in items[j * group:(j + 1) * group]:
                        if it[3] is not None:
                            it[3]()

        def rope_tmps():
            return (walloc(256, BF16), walloc(256, BF16), walloc(512), walloc(512), walloc(512))

        def mixer_A(s, l):
            wreset()
            cosT, sinT = walloc(2048), walloc(2048)
            win = walloc(8 * 768 // 2, BF16, "p (k c) -> p k c", k=8)
            KT = walloc(2048, BF16, "p (j t) -> p j t", j=2)
            VA = walloc(16 * 2 * 192 // 2, BF16, "p (t j c) -> p t j c", t=16, j=2)
            QT = [walloc(3 * 256, BF16, "p (c t) -> p c t", c=3) for _ in range(2)]
            NPT2 = 3
            PT2 = [walloc(512, BF16) for _ in range(NPT2)]
            tmps = rope_tmps()
            rs0 = walloc(512)
            rs = [rs0, rs0]
            dma("sp", cosT, rope_d[0], (), ["rope"], "rope")
            dma("sp", sinT, rope_d[1], (), ["rope"], "rope")
            load_w(win.rearrange("p k c -> p (k c)"), ("in", l, "A"), "winA", "win")
            memset("pool", VA.rearrange("p t j c -> p (t j c)"), 1.0, ["VA"])
            gq, gk = dv[:, l:l + 1], dv[:, 2 + l:3 + l]
            for j in range(2):
                for tb in range(NB):
                    bz = ps_s()
                    zproj(bz, win, 384 + 128 * j, tb, "winA")
                    rope_tile(bz, KT[:, j, tb * 512:(tb + 1) * 512], "KT", tb, gk, True, cosT, sinT, tmps, "ra")
            for t_ in range(NT):
                bv = ps_s()
                vproj(bv, win, 640, 128, t_, "winA")
                copy("act", VA[:, t_, :, 64:128], psb[bv][:, 0:128].rearrange("p (j c) -> p j c", j=2), [("ps", bv)], ["VA"])
            items = []
            pti = [0]

            def mk_pre(qb):
                def pre():
                    qt = QT[qb % 2]
                    for ch in range(3):
                        bz = ps_s()
                        zproj(bz, win, 128 * ch, qb, "winA")
                        rope_tile(bz, qt[:, ch, :], ("QT", qb % 2), qb, gq, True, cosT, sinT, tmps, "ra")
                return pre

            def mk_item(qb, h, kt, st, gst):
                j, ch, half = h // 3, h // 2, h % 2
                pl = slice(half * 64, half * 64 + 64)
                po = slice((1 - half) * 64, (1 - half) * 64 + 64)
                qt = QT[qb % 2]
                qk = ("QT", qb % 2)

                def qk_mm():
                    if half == 0:
                        gst[("b0", kt)] = ps_s2()
                    bs = gst[("b0", kt)] + half
                    mm(psb[bs][:], KT[pl, j, kt * 128:(kt + 1) * 128], qt[pl, ch, :], True, True, ["KT", qk], [("ps", bs)])

                def ex():
                    if half == 0:
                        return
                    b0 = gst[("b0", kt)]
                    i = pti[0] % NPT2
                    pti[0] += 1
                    gst[("pt", kt)] = i
                    act(PT2[i], psS[:, b0 * 512:(b0 + 2) * 512], AF.Exp, [("ps", b0), ("ps", b0 + 1)], [("PT", i)], scale=0.125)

                def pv():
                    if kt == 0:
                        st["ba"] = ps_a()
                    ba = st["ba"]
                    i = gst[("pt", kt)]
                    vl = VA[:, kt, j, 64:192] if half == 0 else VA[:, kt, j, 0:128]
                    mm(psb[ba][:], vl, PT2[i][:, half * 512:(half + 1) * 512], kt == 0, kt == NT - 1, ["VA", ("PT", i)], [("ps", ba)])

                def post():
                    ba = st["ba"]
                    rsx = rs[half]
                    recip(rsx[po, :], psb[ba][po, :], [("ps", ba)], [("rsA", half)])
                    tt("dve", cT[pl, ch, qb * 512:(qb + 1) * 512], psb[ba][pl, :], rsx[po, :], ALU.mult, [("ps", ba), ("rsA", half)], ["cT"])

                return (mk_pre(qb) if (h == 0 and kt == 0) else None, qk_mm, pv, post if kt == NT - 1 else None, ex)

            for qb in range(NB):
                for hp in range(3):
                    sts = [{}, {}]
                    gst_ = {}
                    for kt in range(NT):
                        for hh in range(2):
                            items.append(mk_item(qb, hp * 2 + hh, kt, sts[hh], gst_))
            run_pipeline(items)

        def mixer_B(s, l, jp):
            wreset()
            cosT, sinT = walloc(2048), walloc(2048)
            win = walloc(8 * 384 // 2, BF16, "p (k c) -> p k c", k=8)
            KT = [walloc(1024, BF16) for _ in range(2)]
            VB = walloc(16 * 2 * 192 // 2, BF16, "p (t j c) -> p t j c", t=16, j=2)
            QT = [walloc(256, BF16) for _ in range(2)]
            NPT2 = 3
            PT2 = [walloc(512, BF16) for _ in range(NPT2)]
            tmps = rope_tmps()
            rs = walloc(512)
            o1 = walloc(512)
            o2 = walloc(512)
            ob = walloc(512)
            rb = walloc(512)
            sqb = walloc(256, BF16)
            dma("sp", cosT, rope_d[2], (), ["rope"], "rope")
            dma("sp", sinT, rope_d[3], (), ["rope"], "rope")
            load_w(win.rearrange("p k c -> p (k c)"), ("in", l, "B%d" % jp), "winB", "win")
            memset("pool", VB.rearrange("p t j c -> p (t j c)"), 1.0, ["VB"])
            neglam = dv[:, 8 + l:9 + l]
            sub8 = dv[:, 4 + l:5 + l]
            for tb in range(NB):
                bz = ps_s()
                zproj(bz, win, 128, tb, "winB")
                rope_tile(bz, KT[0][:, tb * 512:(tb + 1) * 512], "KT", tb, None, False, cosT, sinT, tmps, "rb",
                          dst2=KT[1][:, tb * 512:(tb + 1) * 512])
            for t_ in range(NT):
                bv = ps_s()
                vproj(bv, win, 256, 128, t_, "winB")
                copy("act", VB[:, t_, :, 64:128], psb[bv][:, 0:128].rearrange("p (j c) -> p j c", j=2), [("ps", bv)], ["VB"])
            items = []
            pti = [0]
            sc = 32.0 ** -0.5

            def mk_pre(qb):
                def pre():
                    bz = ps_s()
                    zproj(bz, win, 0, qb, "winB")
                    rope_tile(bz, QT[qb % 2], ("QT", qb % 2), qb, None, False, cosT, sinT, tmps, "rb")
                return pre

            def mk_item(qb, hl, comp, kt, st, gst):
                pl = slice(hl * 64, hl * 64 + 64)
                po = slice((1 - hl) * 64, (1 - hl) * 64 + 64)
                qt = QT[qb % 2]
                qk = ("QT", qb % 2)

                def qk_mm():
                    if hl == 0:
                        gst[("b0", comp, kt)] = ps_s2()
                    bs = gst[("b0", comp, kt)] + hl
                    mm(psb[bs][:], KT[comp][pl, kt * 128:(kt + 1) * 128], qt[pl, :], True, True, ["KT", qk], [("ps", bs)])

                def ex():
                    if hl == 0:
                        return
                    b0 = gst[("b0", comp, kt)]
                    i = pti[0] % NPT2
                    pti[0] += 1
                    gst[("pt", comp, kt)] = i
                    act(PT2[i], psS[:, b0 * 512:(b0 + 2) * 512], AF.Exp, [("ps", b0), ("ps", b0 + 1)], [("PT", i)], scale=sc)

                def pv():
                    if kt == 0:
                        st[("ba", comp)] = ps_a()
                    ba = st[("ba", comp)]
                    i = gst[("pt", comp, kt)]
                    vl = VB[:, kt, hl, 64:192] if hl == 0 else VB[:, kt, hl, 0:128]
                    mm(psb[ba][:], vl, PT2[i][:, hl * 512:(hl + 1) * 512], kt == 0, kt == NT - 1, ["VB", ("PT", i)], [("ps", ba)])

                def post():
                    od = (o1, o2)[comp]
                    ba = st[("ba", comp)]
                    recip(rs[po, :], psb[ba][po, :], [("ps", ba)], [("rsB", hl)])
                    tt("dve", od[pl, :], psb[ba][pl, :], rs[po, :], ALU.mult, [("ps", ba), ("rsB", hl)], [("oB", comp, hl)])
                    if comp == 1:
                        stt(ob[pl, :], o2[pl, :], neglam[pl, :], o1[pl, :], ALU.mult, ALU.add, [("oB", 0, hl), ("oB", 1, hl), "dv"], [("obB", hl)])
                        if hl == 1:
                            act(sqb, ob, AF.Square, [("obB", 0), ("obB", 1)], ["sqB"])
                            bn = ps_s()
                            mm(psb[bn][:], ones64_b, sqb, True, True, ["sqB", "mats"], [("ps", bn)])
                            rstd_from_psum(bn, 1.0, 64.0 * EPS, rb, "rB")
                            stt(cT[:, 3 + jp, qb * 512:(qb + 1) * 512], ob, sub8, rb, ALU.mult, ALU.mult, [("obB", 0), ("obB", 1), "rB", "dv"], ["cT"])

                return (mk_pre(qb) if (hl == 0 and comp == 0 and kt == 0) else None, qk_mm, pv,
                        post if kt == NT - 1 else None, ex)

            for qb in range(NB):
                sts = [{}, {}]
                gst_ = {}
                for comp in range(2):
                    for kt in range(NT):
                        for hl in range(2):
                            items.append(mk_item(qb, hl, comp, kt, sts[hl], gst_))
            run_pipeline(items)

        def mixer_C(s, l, jp):
            wreset()
            win = walloc(8 * 640 // 2, BF16, "p (k c) -> p k c", k=8)
            Qd = [walloc(1024, BF16) for _ in range(2)]
            Kd = [walloc(1024, BF16) for _ in range(2)]
            GATE = walloc(1024, BF16)
            VC = walloc(1024, BF16, "p (t c) -> p t c", t=16)
            VCz = walloc(2048, BF16, "p (h t c) -> p h t c", h=2, t=16)
            OF = walloc(2048)
            ET = walloc(128, F32, "p (d c) -> p d c", d=2)
            KTOK = [walloc(64, BF16) for _ in range(2)]
            VM = [walloc(256, BF16, "p (h c e) -> p h c e", h=2, c=4) for _ in range(2)]
            SM = [walloc(128, BF16, "p (h t) -> p h t", h=2) for _ in range(2)]
            BD = [[walloc(256, BF16, "p (c m) -> p c m", c=4) for _ in range(2)] for _ in range(2)]
            LOC = [walloc(256) for _ in range(2)]
            state = [[walloc(64) for _ in range(2)] for _ in range(2)]
            u0 = walloc(64)
            utmp = [[u0, u0], [walloc(64) for _ in range(2)]]
            tmark = wk[0]
            SIL, SG, KKt, BC, EE = [walloc(512) for _ in range(5)]
            load_w(win.rearrange("p k c -> p (k c)"), ("in", l, "C%d" % jp), "winC", "win")
            b3 = 16 + 3 * (l * 2 + jp)
            lb, oml, noml = dv[:, b3:b3 + 1], dv[:, b3 + 1:b3 + 2], dv[:, b3 + 2:b3 + 3]
            hn8 = dv[:, 6 + l:7 + l]
            memset("pool", VCz.rearrange("p h t c -> p (h t c)"), 0.0, ["VCz"])
            for d in range(2):
                for bfi in range(2):
                    memset("pool", BD[d][bfi].rearrange("p c m -> p (c m)"), 0.0, [("BD", d, bfi)])
                memset("pool", state[d][0], 0.0, [("state", d, 0)])
            for tb in range(NB):
                tsl = slice(tb * 512, (tb + 1) * 512)
                bq, bf_, bb_, bg = ps_s(), ps_s(), ps_s(), ps_s()
                zproj(bq, win, 0, tb, "winC")
                zproj(bf_, win, 128, tb, "winC")
                zproj(bb_, win, 256, tb, "winC")
                zproj(bg, win, 384, tb, "winC")
                act(SIL, psb[bq][:], AF.Silu, [("ps", bq)], ["SIL"])
                act(GATE[:, tsl], psb[bg][:], AF.Silu, [("ps", bg)], ["GATE"])
                for d, bz in ((0, bf_), (1, bb_)):
                    act(SG, psb[bz][:], AF.Sigmoid, [("ps", bz)], ["SG"])
                    ts("dve", KKt, SG, noml, oml, ALU.mult, ALU.add, ["SG", "dv"], ["KK"])
                    act(SG, SG, AF.Ln, ["SG", "dv"], ["SG"], scale=oml, bias=lb)
                    ts("dve", SG, SG, LN_FLOOR, None, ALU.max, None, ["SG"], ["SG"])
                    P.op("dve", lambda e: e.tensor_tensor_scan(out=BC, data0=rmask, data1=SG, initial=0.0, op0=ALU.mult, op1=ALU.add),
                         ["SG", "rmask"], ["BC"])
                    act(ET[:, d, tb * 16:(tb + 1) * 16], BC.rearrange("p (c t) -> p c t", t=32)[:, :, 31], AF.Exp, ["BC"], ["ET"])
                    if d == 0:
                        act(EE, BC, AF.Exp, ["BC"], ["EE"])
                        stt(Qd[0][:, tsl], SIL, 0.125, EE, ALU.mult, ALU.mult, ["SIL", "EE"], [("Qd", 0)])
                        act(EE, BC, AF.Exp, ["BC", "EE"], ["EE"], scale=-1.0)
                        tt("dve", Kd[0][:, tsl], KKt, EE, ALU.mult, ["KK", "EE"], [("Kd", 0)])
                    else:
                        tt("dve", BC, BC, SG, ALU.subtract, ["BC", "SG"], ["BC"])
                        act(EE, BC, AF.Exp, ["BC"], ["EE"], scale=-1.0)
                        stt(Qd[1][:, tsl], SIL, 0.125, EE, ALU.mult, ALU.mult, ["SIL", "EE"], [("Qd", 1)])
                        act(EE, BC, AF.Exp, ["BC", "EE"], ["EE"])
                        tt("dve", Kd[1][:, tsl], KKt, EE, ALU.mult, ["KK", "EE"], [("Kd", 1)])
            if stop == "C1":
                raise _Stop()
            for t_ in range(NT):
                bv = ps_s()
                vproj(bv, win, 512, 128, t_, "winC")
                copy("act", VC[:, t_, :], psb[bv][:, 0:128], [("ps", bv)], ["VC"])
                copy("pool", VCz[:, 0, t_, 0:64], VC[:, t_, 0:64], ["VC"], ["VCz"])
                copy("pool", VCz[:, 1, t_, 64:128], VC[:, t_, 64:128], ["VC"], ["VCz"])
            if stop == "C2":
                raise _Stop()
            cstep = [0, 0]
            for it in range(NT):
                bfi = it % 2
                tiles = [it, NT - 1 - it]
                for d in range(2):
                    t_ = tiles[d]
                    tsl = slice(t_ * 128, (t_ + 1) * 128)
                    vm, vmk = VM[d], ("VM", d)
                    for c in range(4):
                        ts("pool", vm[:, :, c, :], VC[:, t_, :].rearrange("p (h e) -> p h e", h=2), bmcol[:, c:c + 1], 1.0, ALU.mult, ALU.mult,
                           ["VC", "bmcol"], [vmk])
                    tcol = ((it * 2 + d) % 8) * 128
                    tkey = "psT"
                    tr(psT[:, tcol:tcol + 128], Kd[d][:, tsl], ident_b, [("Kd", d), "mats"], [tkey])
                    kk_ = ("KTOK", d)
                    copy("act", KTOK[d], psT[:, tcol:tcol + 128], [tkey], [kk_])
                    bl = ps_a()
                    mm(psb[bl][:], KTOK[d], vm.rearrange("p h c e -> p (h c e)"), True, True, [kk_, vmk], [("ps", bl)])
                    lk = ("LOC", d)
                    for hl in range(2):
                        pl = slice(hl * 64, hl * 64 + 64)
                        copy("dve", LOC[d][pl, :], psb[bl][pl, hl * 256:(hl + 1) * 256], [("ps", bl)], [lk])
                    smk = ("SM", d)
                    for hl in range(2):
                        bsc = ps_s()
                        pl = slice(hl * 64, hl * 64 + 64)
                        mm(psb[bsc][:, 0:128], Kd[d][pl, tsl], Qd[d][pl, tsl], True, True, [("Kd", d), ("Qd", d)], [("ps", bsc)])
                        tt("dve", SM[d][:, hl, :], psb[bsc][:, 0:128], maskb[:, d * 256:d * 256 + 128], ALU.mult,
                           [("ps", bsc), "mats"], [smk])
                for ci in range(4):
                    for d in range(2):
                        t_ = tiles[d]
                        c = ci if d == 0 else 3 - ci
                        bd = BD[d][bfi]
                        bdk = ("BD", d, bfi)
                        lk = ("LOC", d)
                        ch = t_ * 4 + c
                        loc = LOC[d][:, c * 64:(c + 1) * 64]
                        et = ET[:, d, ch:ch + 1]
                        n_ = cstep[d]
                        cstep[d] += 1
                        s_prev, s_next = state[d][n_ % 2], state[d][(n_ + 1) % 2]
                        kp, kn = ("state", d, n_ % 2), ("state", d, (n_ + 1) % 2)
                        u = utmp[d][n_ % 2]
                        uk = ("utmp", d, (n_ % 2) if d == 1 else 0)
                        if d == 0:
                            for hl in range(2):
                                pl = slice(hl * 64, hl * 64 + 64)
                                copy("act", bd[pl, c, hl * 64:(hl + 1) * 64], s_prev[pl, :], [kp], [bdk])
                            tt("dve", u, s_prev, loc, ALU.add, [kp, lk], [uk])
                            ts("dve", s_next, u, et, None, ALU.mult, None, [uk, "ET"], [kn])
                        else:
                            ts("dve", u, s_prev, et, None, ALU.mult, None, [kp, "ET"], [uk])
                            for hl in range(2):
                                pl = slice(hl * 64, hl * 64 + 64)
                                copy("act", bd[pl, c, hl * 64:(hl + 1) * 64], u[pl, :], [uk], [bdk])
                            tt("dve", s_next, u, loc, ALU.add, [uk, lk], [kn])
                for d in range(2):
                    t_ = tiles[d]
                    tsl = slice(t_ * 128, (t_ + 1) * 128)
                    bd = BD[d][bfi]
                    bdk = ("BD", d, bfi)
                    smk = ("SM", d)
                    bo = ps_a()
                    for hl in range(2):
                        mm(psb[bo][:, 0:128], VCz[:, hl, t_, :], SM[d][:, hl, :], hl == 0, False, ["VCz", smk], [("ps", bo)])
                    for c in range(4):
                        mm(psb[bo][:, c * 32:(c + 1) * 32], bd[:, c, :], Qd[d][:, t_ * 128 + c * 32:t_ * 128 + (c + 1) * 32], False, c == 3,
                           [bdk, ("Qd", d)], [("ps", bo)])
                    ofk = ("OF", t_)
                    first_visit = (d == 0 and t_ < 8) or (d == 1 and t_ >= 8)
                    if first_visit:
                        copy("act", OF[:, tsl], psb[bo][:, 0:128], [("ps", bo)], [ofk])
                    else:
                        tt("dve", OF[:, tsl], OF[:, tsl], psb[bo][:, 0:128], ALU.add, [("ps", bo), ofk], [ofk])
            if stop == "C3":
                raise _Stop()
            P.barrier()
            wk[0] = tmark
            RB = walloc(512)
            OB = walloc(512)
            SQ = walloc(256, BF16)
            for tb in range(NB):
                tsl = slice(tb * 512, (tb + 1) * 512)
                act(SQ, OF[:, tsl], AF.Square, [("OF", tb * 4 + i) for i in range(4)], ["SQc"])
                bn = ps_s()
                mm(psb[bn][:], ones64_b, SQ, True, True, ["SQc", "mats"], [("ps", bn)])
                rstd_from_psum(bn, 1.0, 64.0 * EPS, RB, "RBc")
                stt(OB, OF[:, tsl], hn8, RB, ALU.mult, ALU.mult, [("OF", tb * 4 + i) for i in range(4)] + ["RBc", "dv"], ["OBc"])
                tt("dve", cT[:, 6 + jp, tsl], OB, GATE[:, tsl], ALU.mult, ["OBc", "GATE"], ["cT"])

        def out_proj(s, l):
            wreset()
            wo = walloc(4096, BF16, "p (h k c) -> p h k c", h=2, k=8)
            for h in range(2):
                load_w(wo[:, h, :, :].rearrange("p k c -> p (k c)"), ("out", l, h), ("wo", h), "wo%d" % h)
            for tb in range(NB):
                tsl = slice(tb * 512, (tb + 1) * 512)
                for m in range(8):
                    bank = ps_s()
                    h, mc = m // 4, (m % 4) * 128
                    for k in range(KC):
                        mm(psb[bank][:], wo[:, h, k, mc:mc + 128], cT[:, k, tsl], k == 0, k == KC - 1, [("wo", h), "cT"], [("ps", bank)])
                    g1 = modT[:, l, 16 + m, s:s + 1]
                    stt(xT[:, m, tsl], psb[bank][:], g1, xT[:, m, tsl], ALU.mult, ALU.add, [("ps", bank), "modT", "xT"], ["xT"])

        def ffn(s, l):
            wreset()
            tmp = norm_tmp()
            norm_to(s, lambda k: modT[:, l, 32 + k, s:s + 1], lambda k: modT[:, l, 24 + k, s:s + 1],
                    lambda k, tb: hT[:, k, tb * 512:(tb + 1) * 512], lambda k, tb: ["hT"], tmp)
            W1 = [walloc(2048, BF16, "p (k c) -> p k c", k=8) for _ in range(2)]
            W2 = [walloc(2048, BF16, "p (k c) -> p k c", k=32) for _ in range(2)]
            rl = [walloc(512) for _ in range(2)]
            wi = 0
            for tb in range(NB):
                tsl = slice(tb * 512, (tb + 1) * 512)
                for j in range(8):
                    w1 = W1[wi % 2]
                    wk_ = ("W1", wi % 2)
                    load_w(w1.rearrange("p k c -> p (k c)"), ("ff1", l, j), wk_, "w1_%d" % (wi % 2))
                    for i in range(4):
                        bank = ps_s()
                        for k in range(KC):
                            mm(psb[bank][:], w1[:, k, i * 128:(i + 1) * 128], hT[:, k, tsl], k == 0, k == KC - 1, [wk_, "hT"], [("ps", bank)])
                        ri = (j * 4 + i) % 2
                        ts("dve", rl[ri], psb[bank][:], 0.0, None, ALU.max, None, [("ps", bank)], [("rl", ri)])
                        tt("pool", aT[:, j * 4 + i, :], rl[ri], rl[ri], ALU.mult, [("rl", ri)], ["aT"])
                    wi += 1
                for m in range(8):
                    w2 = W2[m % 2]
                    wk_ = ("W2", m % 2)
                    load_w(w2.rearrange("p k c -> p (k c)"), ("ff2", l, m), wk_, "w2_%d" % (m % 2))
                    bank = ps_a()
                    for kk in range(32):
                        mm(psb[bank][:], w2[:, kk, :], aT[:, kk, :], kk == 0, kk == 31, [wk_, "aT"], [("ps", bank)])
                    g2 = modT[:, l, 40 + m, s:s + 1]
                    stt(xT[:, m, tsl], psb[bank][:], g2, xT[:, m, tsl], ALU.mult, ALU.add, [("ps", bank), "modT", "xT"], ["xT"])

        def load_x(s):
            wreset()
            xt = [walloc(1024) for _ in range(2)]
            for t_ in range(NT):
                b = t_ % 2
                dma("sp", xt[b], x_d[s, t_ * 128:(t_ + 1) * 128, :], (), [("xtok", b)], "xin%d" % b)
                for g in range(2):
                    bank = ps_s()
                    for kq in range(4):
                        k = g * 4 + kq
                        tr(psb[bank][:, kq * 128:(kq + 1) * 128], xt[b][:, k * 128:(k + 1) * 128], ident_f, [("xtok", b), "ident_f"], [("ps", bank)])
                    copy(["dve", "act"][g], xT[:, g * 4:(g + 1) * 4, t_ * 128:(t_ + 1) * 128],
                         psb[bank][:].rearrange("p (k t) -> p k t", k=4), [("ps", bank)], ["xT"])

        def final_store(s):
            wreset()
            tmp = norm_tmp()
            yT = walloc(4096, F32, "p (k t) -> p k t", k=8)
            yo = [walloc(1024) for _ in range(2)]
            oi = [0]

            def dst(k, tb):
                return yT[:, k, :]

            sqb, rr, tmpf = tmp
            for tb in range(NB):
                tsl = slice(tb * 512, (tb + 1) * 512)
                bank = ps_s()
                for k in range(KC):
                    sk = ("sq", k % 2)
                    if k % 2 == 0:
                        act(sqb[k % 2], xT[:, k, tsl], AF.Square, ["xT"], [sk])
                    else:
                        tt("pool", sqb[k % 2], xT[:, k, tsl], xT[:, k, tsl], ALU.mult, ["xT"], [sk])
                    mm(psb[bank][:], ones_b, sqb[k % 2], k == 0, k == KC - 1, [sk, "mats"], [("ps", bank)])
                rk = ("nr", tb % 2)
                r = rr[tb % 2]
                rstd_from_psum(bank, 1.0 / D, EPS, r, rk)
                for k in range(KC):
                    tk = ("ntmp", k % 2)
                    tt("dve", tmpf[k % 2], xT[:, k, tsl], r, ALU.mult, ["xT", rk], [tk])
                    act(yT[:, k, :], tmpf[k % 2], AF.Copy, [tk, "vecs"], [("yT", k)], scale=vecs[:, 8 + k:9 + k])
                for ti in range(4):
                    t_ = tb * 4 + ti
                    b = oi[0] % 2
                    oi[0] += 1
                    for g in range(2):
                        bank2 = ps_s()
                        for kq in range(4):
                            k = g * 4 + kq
                            tr(psb[bank2][:, kq * 128:(kq + 1) * 128], yT[:, k, ti * 128:(ti + 1) * 128], ident_f, [("yT", k), "ident_f"], [("ps", bank2)])
                        copy(["dve", "act"][g], yo[b][:, g * 512:(g + 1) * 512], psb[bank2][:], [("ps", bank2)], [("yo", b, g)])
                    dma("sp", out_d[s, t_ * 128:(t_ + 1) * 128, :], yo[b], [("yo", b, 0), ("yo", b, 1)], [("yo", b, 0), ("yo", b, 1), "OUT"], "out%d" % b)

        def main():
            if stop == "startup":
                return
            for s in range(nseq):
                load_x(s)
                if stop == "loadx":
                    return
                for l in range(depth):
                    wreset()
                    tmp = norm_tmp()
                    norm_to(s, lambda k: modT[:, l, 8 + k, s:s + 1], lambda k: modT[:, l, k, s:s + 1],
                            lambda k, tb: hT[:, k, tb * 512:(tb + 1) * 512], lambda k, tb: ["hT"], tmp)
                    if debug and s == 0 and l == 0:
                        [dma("pool", dbg["dbg_h"][:, q * 2048:(q + 1) * 2048], V(HT_O + q * 1024, 1024, BF16), ["hT"], ["dbgo"], "dbgp") for q in range(8)]
                    if stop == "norm1":
                        return
                    mixer_A(s, l)
                    if stop == "A":
                        return
                    for jp in range(3):
                        mixer_B(s, l, jp)
                    if stop == "B":
                        return
                    for jp in range(2):
                        mixer_C(s, l, jp)
                    if debug and s == 0 and l == 0:
                        [dma("pool", dbg["dbg_ct"][:, q * 2048:(q + 1) * 2048], V(CT_O + q * 1024, 1024, BF16), ["cT"], ["dbgo"], "dbgp") for q in range(8)]
                    if stop == "C":
                        return
                    out_proj(s, l)
                    if debug and s == 0 and l == 0:
                        [dma("sp", dbg["dbg_xa"][:, q * 2048:(q + 1) * 2048], V(XT_O + q * 2048, 2048), ["xT"], ["dbgo"], "dbg") for q in range(8)]
                    if stop == "outproj":
                        return
                    ffn(s, l)
                    if stop == "ffn":
                        return
                final_store(s)

        try:
            main()
        except _Stop:
            pass
        P.barrier()
        P.op("sp", None, reads=["OUT", "dbgo"])
        P.emit(nc, st)
    return nc, P


def _prep_inputs(inputs, nseq=4, cores=NCORES):
    mats, bmcol, rope = _host_consts()
    f = lambda a: np.ascontiguousarray(np.asarray(a, dtype=np.float32))
    x = f(inputs["x"])
    c = f(inputs["c"])
    p = np.arange(128)
    vecs = np.zeros((128, 32), np.float32)
    for l in range(2):
        vecs[:, l] = inputs["a_qk_norm"][l, 0][p % 64]
        vecs[:, 2 + l] = inputs["a_qk_norm"][l, 1][p % 64]
        vecs[:, 4 + l] = inputs["diff_subln"][l][p % 64]
        vecs[:, 6 + l] = inputs["hgrn_norm"][l][p % 64]
        for cc in range(2):
            vecs[:, 16 + l * 2 + cc] = inputs["hgrn_lower_bounds"][l][cc * 128 + p]
    for k in range(8):
        vecs[:, 8 + k] = inputs["final_norm"][k * 128 + p]
    dlam = np.ascontiguousarray(np.broadcast_to(f(inputs["diff_lambda"]).reshape(1, 256), (128, 256)))
    b_modT = np.ascontiguousarray(f(inputs["b_mod"]).reshape(2, 48, 128).transpose(2, 0, 1).reshape(128, 96))
    shared = {
        "w_mod": f(inputs["w_mod"]), "b_modT": b_modT, "w_in": f(inputs["w_in"]), "w_out": f(inputs["w_out"]),
        "w_ff1": f(inputs["w_ff1"]), "w_ff2": f(inputs["w_ff2"]), "vecs": vecs, "dlam": dlam, "rope": rope,
        "mats": np.ascontiguousarray(mats.reshape(128, 8 * 128)), "bmcol": bmcol,
    }
    maps = []
    for i in range(cores):
        xs = x[i * nseq:(i + 1) * nseq]
        cs = c[i * nseq:(i + 1) * nseq]
        cT = np.ascontiguousarray(cs.T.reshape(8, 128, nseq).transpose(1, 0, 2).reshape(128, 8 * nseq))
        m = dict(shared)
        m["x"] = np.ascontiguousarray(xs)
        m["cT"] = cT
        maps.append(m)
    return maps


_CACHE = {}


def kernel(**inputs):
    nseq = 4
    if "nc" not in _CACHE:
        _CACHE["nc"] = build(nseq=nseq, depth=2, debug=False)[0]
    nc = _CACHE["nc"]
    maps = _prep_inputs(inputs, nseq=nseq)
    res = run_bass_kernel_spmd(nc, maps, core_ids=list(range(NCORES)))
    out = np.concatenate([np.asarray(r["out"]) for r in res.results], axis=0)
    return out.astype(np.float32)
```

```python
import math
import numpy as np
from contextlib import ExitStack
import concourse.bass as bass
import concourse.mybir as mybir
from concourse.bass_utils import run_bass_kernel_spmd

F32 = mybir.dt.float32
BF16 = mybir.dt.bfloat16
ALU = mybir.AluOpType
AF = mybir.ActivationFunctionType

NCORES = 8
S = 2048
D = 1024
NT = 16
NB = 4
KC = 8
EPS = 1e-6
LN_FLOOR = math.log(1e-6)
ENGS = ["pe", "act", "dve", "pool", "sp"]


class _Stop(Exception):
    pass


class Op:
    __slots__ = ("eng", "fn", "deps", "pos", "dma", "dma_cnt", "waits", "signals", "sig_idx", "snap")


class Prog:
    def __init__(self, same_engine_sync=True):
        self.ops = {e: [] for e in ENGS}
        self.order = []
        self.lastw = {}
        self.readers = {}
        self.dma_count = {}
        self.same_engine_sync = same_engine_sync
        self.bar = []
        self.dma_since_bar = []

    def op(self, eng, fn, reads=(), writes=(), dma=None):
        o = Op()
        o.eng = eng
        o.fn = fn
        o.dma = dma
        o.signals = False
        o.sig_idx = 0
        o.waits = []
        o.snap = None
        deps = []
        seen = set()

        def add(d):
            if d is None or id(d) in seen:
                return
            seen.add(id(d))
            if d.dma is None and d.eng == eng:
                if eng in ("pe", "sp"):
                    return
                if not self.same_engine_sync:
                    return
            deps.append(d)

        for d in self.bar:
            add(d)
        for k in reads:
            add(self.lastw.get(k))
        for k in writes:
            add(self.lastw.get(k))
            for r in self.readers.get(k, ()):
                add(r)
        o.deps = deps
        for k in reads:
            self.readers.setdefault(k, []).append(o)
        for k in writes:
            self.lastw[k] = o
            self.readers[k] = []
        o.pos = len(self.ops[eng])
        self.ops[eng].append(o)
        self.order.append(o)
        if dma is not None:
            c = self.dma_count.get(dma, 0) + 1
            self.dma_count[dma] = c
            o.dma_cnt = c
            self.dma_since_bar.append(o)
        else:
            o.dma_cnt = 0
        return o

    def barrier(self):
        bar = []
        for e in ENGS:
            for o in reversed(self.ops[e]):
                if o.dma is None and o.fn is not None:
                    bar.append(o)
                    break
        last_dma = {}
        for o in self.dma_since_bar:
            last_dma[o.dma] = o
        for o in self.bar:
            if o.dma is not None and o.dma not in last_dma:
                last_dma[o.dma] = o
        bar.extend(last_dma.values())
        self.bar = bar
        self.dma_since_bar = []

    def resolve(self):
        known = {e: {} for e in ENGS}
        for o in self.order:
            kn = known[o.eng]
            need = {}
            for d in o.deps:
                if d.dma is not None:
                    key = ("dma", d.dma)
                    val = d.dma_cnt
                else:
                    key = ("eng", d.eng)
                    val = d.pos + 1
                if key not in need or need[key][0] < val:
                    need[key] = (val, d)
            for key, (val, d) in need.items():
                if kn.get(key, 0) >= val:
                    continue
                o.waits.append(d)
                d.signals = True
                kn[key] = val
                if d.snap is not None:
                    for k2, v2 in d.snap.items():
                        if kn.get(k2, 0) < v2:
                            kn[k2] = v2
            o.snap = dict(kn)
        for e in ENGS:
            c = 0
            for o in self.ops[e]:
                if o.dma is None and o.signals:
                    c += 1
                    o.sig_idx = c
        for o in self.order:
            o.snap = None

    def emit(self, nc, stack):
        self.resolve()
        sems = {}
        for e in ENGS:
            sems[("eng", e)] = stack.enter_context(nc.semaphore("s_" + e))
        for name in self.dma_count:
            sems[("dma", name)] = stack.enter_context(nc.semaphore("d_" + str(name)))
        block = stack.enter_context(nc.Block())
        hook = {"pe": block.tensor, "act": block.scalar, "dve": block.vector, "pool": block.gpsimd, "sp": block.sync}

        def make(e):
            def body(eng):
                for o in self.ops[e]:
                    for d in o.waits:
                        if d.dma is not None:
                            eng.wait_ge(sems[("dma", d.dma)], 16 * d.dma_cnt)
                        else:
                            eng.wait_ge(sems[("eng", d.eng)], d.sig_idx)
                    if o.fn is None:
                        continue
                    inst = o.fn(eng)
                    if o.dma is not None:
                        inst.then_inc(sems[("dma", o.dma)], 16)
                    elif o.signals:
                        inst.then_inc(sems[("eng", e)], 1)
            return body

        for e in ENGS:
            if self.ops[e]:
                hook[e](make(e))


def _host_consts():
    p = np.arange(128)
    ones64 = (p[:, None] // 64 == p[None, :] // 64).astype(np.float32)
    partner = np.where(p % 32 < 16, p + 16, p - 16)
    perm = np.zeros((128, 128), np.float32)
    perm[partner, p] = 1.0
    ident = np.eye(128, dtype=np.float32)
    same = (p[:, None] // 32 == p[None, :] // 32)
    mf = (same & (p[:, None] <= p[None, :])).astype(np.float32)
    mb = (same & (p[:, None] >= p[None, :])).astype(np.float32)
    ones = np.ones((128, 128), np.float32)
    mats = np.stack([ones, ones64, perm, ident, mf, mf, mb, mb], axis=1)
    bmcol = np.zeros((128, 8), np.float32)
    bmcol[:, 0:4] = (p[:, None] // 32 == np.arange(4)[None, :])
    bmcol[:, 4] = (p % 64 < 32)
    bmcol[:, 5] = (p % 64 >= 32)
    inv = (np.float32(10000.0) ** (-np.arange(0, 32, 2, dtype=np.float32) / np.float32(32))).astype(np.float32)
    t = np.arange(S)
    row = (t // 64).astype(np.float32)
    col = (t % 64).astype(np.float32)
    tt = t.astype(np.float32)
    j = p % 64
    i16 = (p % 32) % 16
    sign = np.where(p % 32 < 16, -1.0, 1.0).astype(np.float32)
    angA = np.where((j < 32)[:, None], row[None, :] * inv[i16][:, None], col[None, :] * inv[i16][:, None]).astype(np.float32)
    angB = (tt[None, :] * inv[i16][:, None]).astype(np.float32)
    rope = np.stack([np.cos(angA), sign[:, None] * np.sin(angA), np.cos(angB), sign[:, None] * np.sin(angB)], axis=0).astype(np.float32)
    return mats.astype(np.float32), bmcol, rope


AQ, AK, AV, BQ, BK, BV, CQ, CFF, CFB, CI, CG = 0, 384, 512, 640, 1024, 1408, 1792, 2048, 2304, 2560, 2816
IN_PIECES = [("A", [(AQ, 384), (AK, 64), (AK, 64), (AK + 64, 64), (AK + 64, 64), (AV, 128)])]
for _j in range(3):
    IN_PIECES.append(("B%d" % _j, [(BQ + 128 * _j, 128), (BK + 128 * _j, 128), (BV + 128 * _j, 128)]))
for _j in range(2):
    IN_PIECES.append(("C%d" % _j, [(CQ + 128 * _j, 128), (CFF + 128 * _j, 128), (CFB + 128 * _j, 128), (CG + 128 * _j, 128), (CI + 128 * _j, 128)]))
IN_COLS = {n: sum(c for _, c in segs) for n, segs in IN_PIECES}


def build(nseq=4, depth=2, debug=False, same_engine_sync=True, stop=None):
    nc = bass.Bass("TRN2", target_bir_lowering=False)
    P = Prog(same_engine_sync=same_engine_sync)

    def din(name, shape, dt=F32):
        return nc.dram_tensor(name, list(shape), dt, kind="ExternalInput").ap()

    x_d = din("x", [nseq, S, D])
    cT_d = din("cT", [128, 8 * nseq])
    wmod_d = din("w_mod", [2, D, 6 * D])
    bmod_d = din("b_modT", [128, 96])
    win_d = din("w_in", [2, D, 3072])
    wout_d = din("w_out", [2, D, D])
    wff1_d = din("w_ff1", [2, D, 4 * D])
    wff2_d = din("w_ff2", [2, 4 * D, D])
    vecs_d = din("vecs", [128, 32])
    dlam_d = din("dlam", [128, 256])
    rope_d = din("rope", [4, 128, S])
    mats_d = din("mats", [128, 8 * 128])
    bmcol_d = din("bmcol", [128, 8])
    out_d = nc.dram_tensor("out", [nseq, S, D], F32, kind="ExternalOutput").ap()
    dbg = {}
    if debug:
        for nm in ("dbg_ct", "dbg_xa", "dbg_h"):
            dbg[nm] = nc.dram_tensor(nm, [128, 8 * S], F32, kind="ExternalOutput").ap()
        dbg["dbg_mod"] = nc.dram_tensor("dbg_mod", [128, 2 * 48 * nseq], F32, kind="ExternalOutput").ap()
        dbg["dbg_dv"] = nc.dram_tensor("dbg_dv", [128, 64], F32, kind="ExternalOutput").ap()

    scr = {}
    for l in range(depth):
        for n, _ in IN_PIECES:
            scr[("in", l, n)] = nc.dram_tensor("s_in_%d_%s" % (l, n), [128, 8 * IN_COLS[n]], BF16).ap()
        for h in range(2):
            scr[("out", l, h)] = nc.dram_tensor("s_out_%d_%d" % (l, h), [128, 8 * 512], BF16).ap()
        for j in range(8):
            scr[("ff1", l, j)] = nc.dram_tensor("s_ff1_%d_%d" % (l, j), [128, 8 * 512], BF16).ap()
            scr[("ff2", l, j)] = nc.dram_tensor("s_ff2_%d_%d" % (l, j), [128, 32 * 128], BF16).ap()

    with ExitStack() as st:
        ARENA_W = 53200
        arena = st.enter_context(nc.sbuf_tensor("arena", [128, ARENA_W], F32))
        psb = [st.enter_context(nc.psum_tensor("ps%d" % i, [128, 512], F32)) for i in range(7)]
        psT = st.enter_context(nc.psum_tensor("psT", [128, 1024], BF16))

        def V(off, words, dt=F32, pat=None, **kw):
            a = arena[:, off:off + words]
            if dt is not F32:
                a = a.bitcast(dt)
            if pat is not None:
                a = a.rearrange(pat, **kw)
            return a

        XT_O, HT_O, CT_O, CONST_O, WORK_O = 0, 16384, 24576, 32768, 34816
        WORK_W = ARENA_W - WORK_O
        xT = V(XT_O, 16384, F32, "p (k t) -> p k t", k=8)
        hT = V(HT_O, 8192, BF16, "p (k t) -> p k t", k=8)
        cT = V(CT_O, 8192, BF16, "p (k t) -> p k t", k=8)
        aT = V(CT_O, 8192, BF16, "p (k t) -> p k t", k=32)
        co = [CONST_O]

        def calloc(words):
            o = co[0]
            co[0] += words
            assert co[0] <= WORK_O
            return o

        ident_f = V(calloc(128), 128)
        matsb = V(calloc(8 * 64), 8 * 64, BF16, "p (m c) -> p m c", m=8)
        ones_b, ones64_b, perm_b, ident_b = matsb[:, 0, :], matsb[:, 1, :], matsb[:, 2, :], matsb[:, 3, :]
        maskb = V(CONST_O + 128 + 4 * 64, 4 * 64, BF16)
        vecs = V(calloc(32), 32)
        dv = V(calloc(64), 64)
        modT = V(calloc(2 * 48 * nseq), 2 * 48 * nseq, F32, "p (l j s) -> p l j s", l=2, j=48)
        condb = V(calloc(4 * nseq), 4 * nseq, BF16, "p (k s) -> p k s", k=8)
        bmcol = V(calloc(8), 8)
        rmask = V(calloc(512), 512)

        wk = [0]

        def walloc(words, dt=F32, pat=None, **kw):
            o = WORK_O + wk[0]
            wk[0] += words
            assert wk[0] <= WORK_W, ("work overflow", wk[0], WORK_W)
            return V(o, words, dt, pat, **kw)

        def wreset():
            P.barrier()
            wk[0] = 0

        rot = {"s": 0, "a": 0}

        def ps_s():
            i = rot["s"]
            rot["s"] = (i + 1) % 4
            return i

        def ps_a():
            i = 4 + rot["a"]
            rot["a"] = (rot["a"] + 1) % 3
            return i

        uid = [0]

        def K(name):
            uid[0] += 1
            return (name, uid[0])

        def mm(out, lhsT, rhs, start, stop, reads, writes, **kw):
            P.op("pe", lambda e: e.matmul(out, lhsT=lhsT, rhs=rhs, start=start, stop=stop, **kw), reads, writes)

        def tr(out, in_, ident, reads, writes):
            P.op("pe", lambda e: e.transpose(out, in_=in_, identity=ident), reads, writes)

        def act(out, in_, func, reads, writes, scale=1.0, bias=0.0):
            if func is AF.Copy:
                P.op("act", lambda e: e.activation(out=out, in_=in_, func=func, scale=scale), reads, writes)
            else:
                P.op("act", lambda e: e.activation(out=out, in_=in_, func=func, scale=scale, bias=bias), reads, writes)

        def tt(eng, out, in0, in1, op, reads, writes):
            P.op(eng, lambda e: e.tensor_tensor(out=out, in0=in0, in1=in1, op=op), reads, writes)

        def ts(eng, out, in0, s1, s2, op0, op1, reads, writes):
            if op1 is None:
                P.op(eng, lambda e: e.tensor_scalar(out=out, in0=in0, scalar1=s1, scalar2=None, op0=op0), reads, writes)
            else:
                P.op(eng, lambda e: e.tensor_scalar(out=out, in0=in0, scalar1=s1, scalar2=s2, op0=op0, op1=op1), reads, writes)

        def stt(out, in0, scalar, in1, op0, op1, reads, writes):
            P.op("dve", lambda e: e.scalar_tensor_tensor(out=out, in0=in0, scalar=scalar, in1=in1, op0=op0, op1=op1), reads, writes)

        def recip(out, in_, reads, writes):
            P.op("dve", lambda e: e.reciprocal(out=out, in_=in_), reads, writes)

        def copy(eng, out, in_, reads, writes):
            if eng == "act":
                act(out, in_, AF.Copy, reads, writes)
            else:
                P.op(eng, lambda e: e.tensor_copy(out=out, in_=in_), reads, writes)

        def dma(eng, out, in_, reads, writes, sem):
            P.op(eng, lambda e: e.dma_start(out=out, in_=in_), reads, writes, dma=sem)

        def memset(eng, ap, val, writes):
            P.op(eng, lambda e: e.memset(ap, val), (), writes)

        dma("sp", ident_f, mats_d[:, 3 * 128:4 * 128], (), ["ident_f"], "c0")
        dma("pool", V(CONST_O + 128, 8 * 64, BF16), mats_d, (), ["mats"], "c1")
        dma("sp", vecs, vecs_d, (), ["vecs"], "c0")
        dma("sp", bmcol, bmcol_d, (), ["bmcol"], "c0")
        memset("pool", rmask, 1.0, ["rmask"])
        memset("pool", dv, 0.0, ["dv"])
        memset("pool", V(CONST_O + 128 + 512 + 32 + 64, 2 * 48 * nseq), 0.0, ["modT"])
        P.op("pool", lambda e: e.memset(rmask.rearrange("p (c t) -> p c t", t=32)[:, :, 0:1], 0.0), ["rmask"], ["rmask"])

        SU = 20000
        dl = V(SU, 256, F32, "p (l a j) -> p l a j", l=2, a=4)
        bmodT = V(SU + 256, 96)
        cond_f = V(SU + 352, 8 * nseq)
        small = V(SU + 352 + 8 * nseq, 64)
        dma("sp", V(SU, 256), dlam_d, (), ["dl"], "c0")
        dma("sp", bmodT, bmod_d, (), ["bmodT"], "c0")
        dma("sp", cond_f, cT_d, (), ["cond_f"], "c0")
        P.barrier()
        act(V(CONST_O + 128 + 512 + 32 + 64 + 2 * 48 * nseq, 4 * nseq, BF16), cond_f, AF.Silu, ["cond_f"], ["condb"])

        for l in range(2):
            lam_init = 0.8 - 0.6 * math.exp(-0.3 * l)
            ts("dve", dv[:, l:l + 1], vecs[:, l:l + 1], 8.0, None, ALU.mult, None, ["vecs"], ["dv"])
            ts("dve", dv[:, 2 + l:3 + l], vecs[:, 2 + l:3 + l], 8.0, None, ALU.mult, None, ["vecs"], ["dv"])
            ts("dve", dv[:, 4 + l:5 + l], vecs[:, 4 + l:5 + l], 8.0 * (1.0 - lam_init), None, ALU.mult, None, ["vecs"], ["dv"])
            ts("dve", dv[:, 6 + l:7 + l], vecs[:, 6 + l:7 + l], 8.0, None, ALU.mult, None, ["vecs"], ["dv"])
            tt("dve", small[:, 0:32], dl[:, l, 0, :], dl[:, l, 1, :], ALU.mult, ["dl"], ["small"])
            P.op("dve", lambda e: e.reduce_sum(out=small[:, 32:33], in_=small[:, 0:32], axis=mybir.AxisListType.X), ["small"], ["small"])
            tt("dve", small[:, 0:32], dl[:, l, 2, :], dl[:, l, 3, :], ALU.mult, ["dl", "small"], ["small"])
            P.op("dve", lambda e: e.reduce_sum(out=small[:, 33:34], in_=small[:, 0:32], axis=mybir.AxisListType.X), ["small"], ["small"])
            act(small[:, 34:36], small[:, 32:34], AF.Exp, ["small"], ["small"])
            ts("dve", dv[:, 8 + l:9 + l], small[:, 35:36], small[:, 34:35], -lam_init, ALU.subtract, ALU.add, ["small"], ["dv"])
        h0, h1 = vecs[:, 16:18], vecs[:, 18:20]
        sm_m, sm_e0, sm_e1, sm_s, sm_p0, sm_p1 = [small[:, 40 + 2 * i:42 + 2 * i] for i in range(6)]
        tt("dve", sm_m, h0, h1, ALU.max, ["vecs", "small"], ["small"])
        tt("dve", sm_e0, h0, sm_m, ALU.subtract, ["small"], ["small"])
        tt("dve", sm_e1, h1, sm_m, ALU.subtract, ["small"], ["small"])
        act(sm_e0, sm_e0, AF.Exp, ["small"], ["small"])
        act(sm_e1, sm_e1, AF.Exp, ["small"], ["small"])
        tt("dve", sm_s, sm_e0, sm_e1, ALU.add, ["small"], ["small"])
        recip(sm_s, sm_s, ["small"], ["small"])
        tt("dve", sm_p0, sm_e0, sm_s, ALU.mult, ["small"], ["small"])
        tt("dve", sm_p1, sm_e1, sm_s, ALU.mult, ["small"], ["small"])
        lbv = small[:, 52:56]
        tt("dve", lbv[:, 0:2], sm_p0, sm_p0, ALU.subtract, ["small"], ["small"])
        tt("dve", lbv[:, 2:4], sm_p0, sm_p1, ALU.add, ["small"], ["small"])
        tt("dve", lbv[:, 2:4], lbv[:, 2:4], sm_p0, ALU.subtract, ["small"], ["small"])
        ts("dve", lbv, lbv, 0.0, 1.0, ALU.max, ALU.min, ["small"], ["small"])
        for i in range(4):
            b = 16 + 3 * i
            copy("dve", dv[:, b:b + 1], lbv[:, i:i + 1], ["small"], ["dv"])
            ts("dve", dv[:, b + 1:b + 2], lbv[:, i:i + 1], -1.0, 1.0, ALU.mult, ALU.add, ["small"], ["dv"])
            ts("dve", dv[:, b + 2:b + 3], lbv[:, i:i + 1], -1.0, None, ALU.add, None, ["small"], ["dv"])

        MSTG = 0
        for l in range(depth):
            wv = wmod_d[l].rearrange("(k p) c -> p k c", p=128)
            for g in range(8):
                stg = V(MSTG + (g % 2) * 6144, 6144, F32, "p (k c) -> p k c", k=8)
                wb = V(MSTG + 12288 + (g % 2) * 3072, 3072, BF16, "p (k c) -> p k c", k=8)
                ks, kb = ("mstg", g % 2), ("mwb", g % 2)
                dma("sp", stg, wv[:, :, g * 768:(g + 1) * 768], (), [ks], "mod%d" % (g % 2))
                copy(["dve", "pool"][g % 2], wb, stg, [ks], [kb])
                bank = ps_s()
                for jj in range(6):
                    for k in range(8):
                        mm(psb[bank][:, jj * nseq:(jj + 1) * nseq], wb[:, k, jj * 128:(jj + 1) * 128], condb[:, k, :], k == 0, k == 7,
                           [kb, "condb"], [("ps", bank)])
                for jj in range(6):
                    j = g * 6 + jj
                    add1 = 1.0 if (8 <= j < 16 or 32 <= j < 40) else 0.0
                    ts("dve", modT[:, l, j, :], psb[bank][:, jj * nseq:(jj + 1) * nseq], bmodT[:, l * 48 + j:l * 48 + j + 1], add1,
                       ALU.add, ALU.add, [("ps", bank), "bmodT"], ["modT"])

        P.barrier()
        CS = 0
        cast_i = [0]

        def cast_piece(dst, segs, kc, cols):
            i = cast_i[0]
            cast_i[0] += 1
            b = i % 2
            stg = V(CS + b * 6144, kc * cols, F32, "p (k c) -> p k c", k=kc)
            wb = V(CS + 12288 + b * 3072, (kc * cols) // 2, BF16)
            ks, kb = ("cstg", b), ("cwb", b)
            o = 0
            for src, c in segs:
                dma("sp", stg[:, :, o:o + c], src, (), [ks], "cl%d" % b)
                o += c
            eng = ["dve", "pool", "act"][i % 3]
            copy(eng, wb, V(CS + b * 6144, kc * cols, F32), [ks], [kb])
            dma("sp", dst, wb, [kb], [("scr", id(dst))], "cs%d" % b)
            return ("scr", id(dst))

        scr_key = {}
        for l in range(depth):
            wv = win_d[l].rearrange("(k p) c -> p k c", p=128)
            for n, segs in IN_PIECES:
                scr_key[("in", l, n)] = cast_piece(scr[("in", l, n)], [(wv[:, :, c0:c0 + c], c) for c0, c in segs], 8, IN_COLS[n])
            wv = wout_d[l].rearrange("(k p) c -> p k c", p=128)
            for h in range(2):
                scr_key[("out", l, h)] = cast_piece(scr[("out", l, h)], [(wv[:, :, h * 512:(h + 1) * 512], 512)], 8, 512)
            wv = wff1_d[l].rearrange("(k p) c -> p k c", p=128)
            for j in range(8):
                scr_key[("ff1", l, j)] = cast_piece(scr[("ff1", l, j)], [(wv[:, :, j * 512:(j + 1) * 512], 512)], 8, 512)
            wv = wff2_d[l].rearrange("(k p) c -> p k c", p=128)
            for g in range(8):
                i = cast_i[0]
                cast_i[0] += 1
                b = i % 2
                stg = V(CS + b * 6144, 4096, F32, "p (k c) -> p k c", k=4)
                wb = V(CS + 12288 + b * 3072, 2048, BF16, "p (k c) -> p k c", k=4)
                ks, kb = ("cstg", b), ("cwb", b)
                dma("sp", stg, wv[:, g * 4:(g + 1) * 4, :], (), [ks], "cl%d" % b)
                copy(["dve", "pool", "act"][i % 3], V(CS + 12288 + b * 3072, 2048, BF16), V(CS + b * 6144, 4096, F32), [ks], [kb])
                for j in range(8):
                    dst = scr[("ff2", l, j)].rearrange("p (k c) -> p k c", k=32)[:, g * 4:(g + 1) * 4, :]
                    dma("sp", dst, wb[:, :, j * 128:(j + 1) * 128], [kb], [K("scrff2")], "cs%d" % b)
                    scr_key[("ff2", l, j)] = ("scr_ff2_unused", l, j)

        if debug:
            dma("sp", dbg["dbg_mod"], V(CONST_O + 128 + 512 + 32 + 64, 2 * 48 * nseq), ["modT"], ["dbgo"], "dbg")
            dma("sp", dbg["dbg_dv"], dv, ["dv"], ["dbgo"], "dbg")

        def rstd_from_psum(bank, scale, bias, r_ap, rk):
            act(r_ap, psb[bank][:], AF.Sqrt, [("ps", bank)], [rk], scale=scale, bias=bias)
            recip(r_ap, r_ap, [rk], [rk])

        def norm_to(s, get_scale, get_bias, dst_fn, dst_keys_fn, tmp):
            sqb, rr, tmpf = tmp
            for tb in range(NB):
                tsl = slice(tb * 512, (tb + 1) * 512)
                bank = ps_s()
                for k in range(KC):
                    sk = ("sq", k % 2)
                    if k % 2 == 0:
                        act(sqb[k % 2], xT[:, k, tsl], AF.Square, ["xT"], [sk])
                    else:
                        tt("pool", sqb[k % 2], xT[:, k, tsl], xT[:, k, tsl], ALU.mult, ["xT"], [sk])
                    mm(psb[bank][:], ones_b, sqb[k % 2], k == 0, k == KC - 1, [sk, "mats"], [("ps", bank)])
                rk = ("nr", tb % 2)
                r = rr[tb % 2]
                rstd_from_psum(bank, 1.0 / D, EPS, r, rk)
                for k in range(KC):
                    tk = ("ntmp", k % 2)
                    tt("dve", tmpf[k % 2], xT[:, k, tsl], r, ALU.mult, ["xT", rk], [tk])
                    bias = get_bias(k)
                    out_ap = dst_fn(k, tb)
                    if bias is None:
                        act(out_ap, tmpf[k % 2], AF.Copy, [tk, "modT", "vecs"], dst_keys_fn(k, tb), scale=get_scale(k))
                    else:
                        act(out_ap, tmpf[k % 2], AF.Identity, [tk, "modT", "vecs"], dst_keys_fn(k, tb), scale=get_scale(k), bias=bias)

        def norm_tmp():
            sqb = [walloc(256, BF16) for _ in range(2)]
            rr = [walloc(512) for _ in range(2)]
            tmpf = [walloc(512) for _ in range(2)]
            return sqb, rr, tmpf

        def load_w(dst, key_scr, name, sem):
            dma("sp", dst, scr[key_scr], [scr_key[key_scr]], [name], sem)

        def zproj(bank, w, c0, tb, wkey):
            tsl = slice(tb * 512, (tb + 1) * 512)
            for k in range(KC):
                mm(psb[bank][:], w[:, k, c0:c0 + 128], hT[:, k, tsl], k == 0, k == KC - 1, [wkey, "hT"], [("ps", bank)])

        def vproj(bank, w, c0, ncols, tt_, wkey, col_off=0):
            tsl = slice(tt_ * 128, (tt_ + 1) * 128)
            for k in range(KC):
                mm(psb[bank][:, col_off:col_off + ncols], hT[:, k, tsl], w[:, k, c0:c0 + ncols], k == 0, k == KC - 1, [wkey, "hT"], [("ps", bank)])

        def rope_tile(bank_z, dst, dkey, tb, gain, do_norm, cosT, sinT, tmps, tag, dst2=None):
            a_b, sq_b, t1, t2, r = tmps
            tsl = slice(tb * 512, (tb + 1) * 512)
            ka, ksq, kt1, kt2, kr = [(tag, n) for n in ("a", "sq", "t1", "t2", "r")]
            if gain is None:
                act(a_b, psb[bank_z][:], AF.Copy, [("ps", bank_z)], [ka])
            else:
                act(a_b, psb[bank_z][:], AF.Copy, [("ps", bank_z), "dv"], [ka], scale=gain)
            bank_r = ps_s()
            mm(psb[bank_r][:], perm_b, a_b, True, True, [ka, "mats"], [("ps", bank_r)])
            if do_norm:
                act(sq_b, psb[bank_z][:], AF.Square, [("ps", bank_z)], [ksq])
                bank_n = ps_s()
                mm(psb[bank_n][:], ones64_b, sq_b, True, True, [ksq, "mats"], [("ps", bank_n)])
                rstd_from_psum(bank_n, 1.0, 64.0 * EPS, r, kr)
            tt("dve", t1, a_b, cosT[:, tsl], ALU.mult, [ka, "rope"], [kt1])
            tt("dve", t2, psb[bank_r][:], sinT[:, tsl], ALU.mult, [("ps", bank_r), "rope"], [kt2])
            if do_norm:
                tt("pool", t1, t1, t2, ALU.add, [kt1, kt2], [kt1])
                tt("dve", dst, t1, r, ALU.mult, [kt1, kr], [dkey])
            elif dst2 is None:
                tt("pool", dst, t1, t2, ALU.add, [kt1, kt2], [dkey])
            else:
                tt("pool", t1, t1, t2, ALU.add, [kt1, kt2], [kt1])
                ts("dve", dst, t1, bmcol[:, 4:5], None, ALU.mult, None, [kt1, "bmcol"], [dkey])
                ts("pool", dst2, t1, bmcol[:, 5:6], 1.0, ALU.mult, ALU.mult, [kt1, "bmcol"], [dkey])

        def run_pipeline(items, group=2, lag=1):
            n = len(items)
            assert n % group == 0
            ng = n // group
            AHEAD = 20
            pre_at = {}
            for i, it in enumerate(items):
                if it[0] is not None:
                    pre_at.setdefault(max(0, i // group - AHEAD), []).append(it[0])
            for step in range(ng + lag):
                for f in pre_at.get(step, ()):
                    f()
                if step < ng:
                    for it in items[step * group:(step + 1) * group]:
                        it[1]()
                    for it in items[step * group:(step + 1) * group]:
                        it[4]()
                j = step - lag
                if j >= 0:
                    for it in items[j * group:(j + 1) * group]:
                        it[2]()
                    for it in items[j * group:(j + 1) * group]:
                        if it[3] is not None:
                            it[3]()

        def rope_tmps():
            return (walloc(256, BF16), walloc(256, BF16), walloc(512), walloc(512), walloc(512))

        def mixer_A(s, l):
            wreset()
            cosT, sinT = walloc(2048), walloc(2048)
            win = walloc(8 * 768 // 2, BF16, "p (k c) -> p k c", k=8)
            KT = walloc(2048, BF16, "p (j t) -> p j t", j=2)
            VA = walloc(16 * 2 * 192 // 2, BF16, "p (t j c) -> p t j c", t=16, j=2)
            QT = [walloc(3 * 256, BF16, "p (c t) -> p c t", c=3) for _ in range(2)]
            NPT = 6
            PT = [walloc(256, BF16) for _ in range(NPT)]
            tmps = rope_tmps()
            rs0 = walloc(512)
            rs = [rs0, rs0]
            dma("sp", cosT, rope_d[0], (), ["rope"], "rope")
            dma("sp", sinT, rope_d[1], (), ["rope"], "rope")
            load_w(win.rearrange("p k c -> p (k c)"), ("in", l, "A"), "winA", "win")
            memset("pool", VA.rearrange("p t j c -> p (t j c)"), 1.0, ["VA"])
            gq, gk = dv[:, l:l + 1], dv[:, 2 + l:3 + l]
            for j in range(2):
                for tb in range(NB):
                    bz = ps_s()
                    zproj(bz, win, 384 + 128 * j, tb, "winA")
                    rope_tile(bz, KT[:, j, tb * 512:(tb + 1) * 512], "KT", tb, gk, True, cosT, sinT, tmps, "ra")
            for t_ in range(NT):
                bv = ps_s()
                vproj(bv, win, 640, 128, t_, "winA")
                copy("act", VA[:, t_, :, 64:128], psb[bv][:, 0:128].rearrange("p (j c) -> p j c", j=2), [("ps", bv)], ["VA"])
            items = []
            pti = [0]

            def mk_pre(qb):
                def pre():
                    qt = QT[qb % 2]
                    for ch in range(3):
                        bz = ps_s()
                        zproj(bz, win, 128 * ch, qb, "winA")
                        rope_tile(bz, qt[:, ch, :], ("QT", qb % 2), qb, gq, True, cosT, sinT, tmps, "ra")
                return pre

            def mk_item(qb, h, kt, st):
                j, ch, half = h // 3, h // 2, h % 2
                pl = slice(half * 64, half * 64 + 64)
                po = slice((1 - half) * 64, (1 - half) * 64 + 64)
                qt = QT[qb % 2]
                qk = ("QT", qb % 2)

                def qk_mm():
                    bs = ps_s()
                    st[("bs", kt)] = bs
                    mm(psb[bs][:], KT[pl, j, kt * 128:(kt + 1) * 128], qt[pl, ch, :], True, True, ["KT", qk], [("ps", bs)])

                def ex():
                    bs = st[("bs", kt)]
                    i = pti[0] % NPT
                    pti[0] += 1
                    st[("pt", kt)] = i
                    act(PT[i], psb[bs][:], AF.Exp, [("ps", bs)], [("PT", i)], scale=0.125)

                def pv():
                    if kt == 0:
                        st["ba"] = ps_a()
                    ba = st["ba"]
                    i = st[("pt", kt)]
                    vl = VA[:, kt, j, 64:192] if half == 0 else VA[:, kt, j, 0:128]
                    mm(psb[ba][:], vl, PT[i], kt == 0, kt == NT - 1, ["VA", ("PT", i)], [("ps", ba)])

                def post():
                    ba = st["ba"]
                    rsx = rs[half]
                    recip(rsx[po, :], psb[ba][po, :], [("ps", ba)], [("rsA", half)])
                    tt("dve", cT[pl, ch, qb * 512:(qb + 1) * 512], psb[ba][pl, :], rsx[po, :], ALU.mult, [("ps", ba), ("rsA", half)], ["cT"])

                return (mk_pre(qb) if (h == 0 and kt == 0) else None, qk_mm, pv, post if kt == NT - 1 else None, ex)

            for qb in range(NB):
                for hp in range(3):
                    sts = [{}, {}]
                    for kt in range(NT):
                        for hh in range(2):
                            items.append(mk_item(qb, hp * 2 + hh, kt, sts[hh]))
            run_pipeline(items)

        Bst = {}

        def mixer_B(s, l, jp):
            if jp == 0:
                wreset()
                Bst.clear()
                Bst["cos"], Bst["sin"] = walloc(2048), walloc(2048)
                Bst["win"] = walloc(8 * 384 // 2, BF16, "p (k c) -> p k c", k=8)
                Bst["KT"] = [walloc(1024, BF16) for _ in range(2)]
                Bst["VB"] = walloc(16 * 2 * 192 // 2, BF16, "p (t j c) -> p t j c", t=16, j=2)
                Bst["QT"] = [walloc(256, BF16) for _ in range(2)]
                Bst["PT"] = [walloc(256, BF16) for _ in range(6)]
                Bst["tmps"] = rope_tmps()
                Bst["misc"] = [walloc(512) for _ in range(5)]
                Bst["sqb"] = walloc(256, BF16)
                dma("sp", Bst["cos"], rope_d[2], (), ["rope"], "rope")
                dma("sp", Bst["sin"], rope_d[3], (), ["rope"], "rope")
                memset("pool", Bst["VB"].rearrange("p t j c -> p (t j c)"), 1.0, ["VB"])
            cosT, sinT, win, KT, VB, QT, PT, tmps, sqb = (Bst[k] for k in ("cos", "sin", "win", "KT", "VB", "QT", "PT", "tmps", "sqb"))
            NPT = 6
            rs, o1, o2, ob, rb = Bst["misc"]
            load_w(win.rearrange("p k c -> p (k c)"), ("in", l, "B%d" % jp), "winB", "win")
            neglam = dv[:, 8 + l:9 + l]
            sub8 = dv[:, 4 + l:5 + l]
            for tb in range(NB):
                bz = ps_s()
                zproj(bz, win, 128, tb, "winB")
                rope_tile(bz, KT[0][:, tb * 512:(tb + 1) * 512], "KT", tb, None, False, cosT, sinT, tmps, "rb",
                          dst2=KT[1][:, tb * 512:(tb + 1) * 512])
            for t_ in range(NT):
                bv = ps_s()
                vproj(bv, win, 256, 128, t_, "winB")
                copy("act", VB[:, t_, :, 64:128], psb[bv][:, 0:128].rearrange("p (j c) -> p j c", j=2), [("ps", bv)], ["VB"])
            items = []
            pti = [0]
            sc = 32.0 ** -0.5

            def mk_pre(qb):
                def pre():
                    bz = ps_s()
                    zproj(bz, win, 0, qb, "winB")
                    rope_tile(bz, QT[qb % 2], ("QT", qb % 2), qb, None, False, cosT, sinT, tmps, "rb")
                return pre

            def mk_item(qb, hl, comp, kt, st):
                pl = slice(hl * 64, hl * 64 + 64)
                po = slice((1 - hl) * 64, (1 - hl) * 64 + 64)
                qt = QT[qb % 2]
                qk = ("QT", qb % 2)

                def qk_mm():
                    bs = ps_s()
                    st[("bs", comp, kt)] = bs
                    mm(psb[bs][:], KT[comp][pl, kt * 128:(kt + 1) * 128], qt[pl, :], True, True, ["KT", qk], [("ps", bs)])

                def ex():
                    bs = st[("bs", comp, kt)]
                    i = pti[0] % NPT
                    pti[0] += 1
                    st[("pt", comp, kt)] = i
                    act(PT[i], psb[bs][:], AF.Exp, [("ps", bs)], [("PT", i)], scale=sc)

                def pv():
                    if kt == 0:
                        st[("ba", comp)] = ps_a()
                    ba = st[("ba", comp)]
                    i = st[("pt", comp, kt)]
                    vl = VB[:, kt, hl, 64:192] if hl == 0 else VB[:, kt, hl, 0:128]
                    mm(psb[ba][:], vl, PT[i], kt == 0, kt == NT - 1, ["VB", ("PT", i)], [("ps", ba)])

                def post():
                    od = (o1, o2)[comp]
                    ba = st[("ba", comp)]
                    recip(rs[po, :], psb[ba][po, :], [("ps", ba)], [("rsB", hl)])
                    tt("dve", od[pl, :], psb[ba][pl, :], rs[po, :], ALU.mult, [("ps", ba), ("rsB", hl)], [("oB", comp, hl)])
                    if comp == 1:
                        stt(ob[pl, :], o2[pl, :], neglam[pl, :], o1[pl, :], ALU.mult, ALU.add, [("oB", 0, hl), ("oB", 1, hl), "dv"], [("obB", hl)])
                        if hl == 1:
                            act(sqb, ob, AF.Square, [("obB", 0), ("obB", 1)], ["sqB"])
                            bn = ps_s()
                            mm(psb[bn][:], ones64_b, sqb, True, True, ["sqB", "mats"], [("ps", bn)])
                            rstd_from_psum(bn, 1.0, 64.0 * EPS, rb, "rB")
                            stt(cT[:, 3 + jp, qb * 512:(qb + 1) * 512], ob, sub8, rb, ALU.mult, ALU.mult, [("obB", 0), ("obB", 1), "rB", "dv"], ["cT"])

                return (mk_pre(qb) if (hl == 0 and comp == 0 and kt == 0) else None, qk_mm, pv,
                        post if kt == NT - 1 else None, ex)

            for qb in range(NB):
                sts = [{}, {}]
                for comp in range(2):
                    for kt in range(NT):
                        for hl in range(2):
                            items.append(mk_item(qb, hl, comp, kt, sts[hl]))
            run_pipeline(items)

        def mixer_C(s, l, jp):
            wreset()
            win = walloc(8 * 640 // 2, BF16, "p (k c) -> p k c", k=8)
            Qd = [walloc(1024, BF16) for _ in range(2)]
            Kd = [walloc(1024, BF16) for _ in range(2)]
            GATE = walloc(1024, BF16)
            VC = walloc(1024, BF16, "p (t c) -> p t c", t=16)
            VCz = walloc(2048, BF16, "p (h t c) -> p h t c", h=2, t=16)
            OF = walloc(2048)
            ET = walloc(128, F32, "p (d c) -> p d c", d=2)
            KTOK = [walloc(64, BF16) for _ in range(2)]
            VM = [walloc(256, BF16, "p (h c e) -> p h c e", h=2, c=4) for _ in range(2)]
            SM = [walloc(128, BF16, "p (h t) -> p h t", h=2) for _ in range(2)]
            BD = [[walloc(256, BF16, "p (c m) -> p c m", c=4) for _ in range(2)] for _ in range(2)]
            LOC = [walloc(256) for _ in range(2)]
            state = [[walloc(64) for _ in range(2)] for _ in range(2)]
            u0 = walloc(64)
            utmp = [[u0, u0], [walloc(64) for _ in range(2)]]
            tmark = wk[0]
            SIL, SG, KKt, BC, EE = [walloc(512) for _ in range(5)]
            load_w(win.rearrange("p k c -> p (k c)"), ("in", l, "C%d" % jp), "winC", "win")
            b3 = 16 + 3 * (l * 2 + jp)
            lb, oml, noml = dv[:, b3:b3 + 1], dv[:, b3 + 1:b3 + 2], dv[:, b3 + 2:b3 + 3]
            hn8 = dv[:, 6 + l:7 + l]
            memset("pool", VCz.rearrange("p h t c -> p (h t c)"), 0.0, ["VCz"])
            for d in range(2):
                for bfi in range(2):
                    memset("pool", BD[d][bfi].rearrange("p c m -> p (c m)"), 0.0, [("BD", d, bfi)])
                memset("pool", state[d][0], 0.0, [("state", d, 0)])
            for tb in range(NB):
                tsl = slice(tb * 512, (tb + 1) * 512)
                bq, bf_, bb_, bg = ps_s(), ps_s(), ps_s(), ps_s()
                zproj(bq, win, 0, tb, "winC")
                zproj(bf_, win, 128, tb, "winC")
                zproj(bb_, win, 256, tb, "winC")
                zproj(bg, win, 384, tb, "winC")
                act(SIL, psb[bq][:], AF.Silu, [("ps", bq)], ["SIL"])
                act(GATE[:, tsl], psb[bg][:], AF.Silu, [("ps", bg)], ["GATE"])
                for d, bz in ((0, bf_), (1, bb_)):
                    act(SG, psb[bz][:], AF.Sigmoid, [("ps", bz)], ["SG"])
                    ts("dve", KKt, SG, noml, oml, ALU.mult, ALU.add, ["SG", "dv"], ["KK"])
                    act(SG, SG, AF.Ln, ["SG", "dv"], ["SG"], scale=oml, bias=lb)
                    ts("dve", SG, SG, LN_FLOOR, None, ALU.max, None, ["SG"], ["SG"])
                    P.op("dve", lambda e: e.tensor_tensor_scan(out=BC, data0=rmask, data1=SG, initial=0.0, op0=ALU.mult, op1=ALU.add),
                         ["SG", "rmask"], ["BC"])
                    act(ET[:, d, tb * 16:(tb + 1) * 16], BC.rearrange("p (c t) -> p c t", t=32)[:, :, 31], AF.Exp, ["BC"], ["ET"])
                    if d == 0:
                        act(EE, BC, AF.Exp, ["BC"], ["EE"])
                        stt(Qd[0][:, tsl], SIL, 0.125, EE, ALU.mult, ALU.mult, ["SIL", "EE"], [("Qd", 0)])
                        act(EE, BC, AF.Exp, ["BC", "EE"], ["EE"], scale=-1.0)
                        tt("dve", Kd[0][:, tsl], KKt, EE, ALU.mult, ["KK", "EE"], [("Kd", 0)])
                    else:
                        tt("dve", BC, BC, SG, ALU.subtract, ["BC", "SG"], ["BC"])
                        act(EE, BC, AF.Exp, ["BC"], ["EE"], scale=-1.0)
                        stt(Qd[1][:, tsl], SIL, 0.125, EE, ALU.mult, ALU.mult, ["SIL", "EE"], [("Qd", 1)])
                        act(EE, BC, AF.Exp, ["BC", "EE"], ["EE"])
                        tt("dve", Kd[1][:, tsl], KKt, EE, ALU.mult, ["KK", "EE"], [("Kd", 1)])
            if stop == "C1":
                raise _Stop()
            for t_ in range(NT):
                bv = ps_s()
                vproj(bv, win, 512, 128, t_, "winC")
                copy("act", VC[:, t_, :], psb[bv][:, 0:128], [("ps", bv)], ["VC"])
                copy("pool", VCz[:, 0, t_, 0:64], VC[:, t_, 0:64], ["VC"], ["VCz"])
                copy("pool", VCz[:, 1, t_, 64:128], VC[:, t_, 64:128], ["VC"], ["VCz"])
            if stop == "C2":
                raise _Stop()
            cstep = [0, 0]
            for it in range(NT):
                bfi = it % 2
                tiles = [it, NT - 1 - it]
                for d in range(2):
                    t_ = tiles[d]
                    tsl = slice(t_ * 128, (t_ + 1) * 128)
                    vm, vmk = VM[d], ("VM", d)
                    for c in range(4):
                        ts("pool", vm[:, :, c, :], VC[:, t_, :].rearrange("p (h e) -> p h e", h=2), bmcol[:, c:c + 1], 1.0, ALU.mult, ALU.mult,
                           ["VC", "bmcol"], [vmk])
                    tcol = ((it * 2 + d) % 8) * 128
                    tkey = "psT"
                    tr(psT[:, tcol:tcol + 128], Kd[d][:, tsl], ident_b, [("Kd", d), "mats"], [tkey])
                    kk_ = ("KTOK", d)
                    copy("act", KTOK[d], psT[:, tcol:tcol + 128], [tkey], [kk_])
                    bl = ps_a()
                    mm(psb[bl][:], KTOK[d], vm.rearrange("p h c e -> p (h c e)"), True, True, [kk_, vmk], [("ps", bl)])
                    lk = ("LOC", d)
                    for hl in range(2):
                        pl = slice(hl * 64, hl * 64 + 64)
                        copy("dve", LOC[d][pl, :], psb[bl][pl, hl * 256:(hl + 1) * 256], [("ps", bl)], [lk])
                    smk = ("SM", d)
                    for hl in range(2):
                        bsc = ps_s()
                        pl = slice(hl * 64, hl * 64 + 64)
                        mm(psb[bsc][:, 0:128], Kd[d][pl, tsl], Qd[d][pl, tsl], True, True, [("Kd", d), ("Qd", d)], [("ps", bsc)])
                        tt("dve", SM[d][:, hl, :], psb[bsc][:, 0:128], maskb[:, d * 256:d * 256 + 128], ALU.mult,
                           [("ps", bsc), "mats"], [smk])
                for ci in range(4):
                    for d in range(2):
                        t_ = tiles[d]
                        c = ci if d == 0 else 3 - ci
                        bd = BD[d][bfi]
                        bdk = ("BD", d, bfi)
                        lk = ("LOC", d)
                        ch = t_ * 4 + c
                        loc = LOC[d][:, c * 64:(c + 1) * 64]
                        et = ET[:, d, ch:ch + 1]
                        n_ = cstep[d]
                        cstep[d] += 1
                        s_prev, s_next = state[d][n_ % 2], state[d][(n_ + 1) % 2]
                        kp, kn = ("state", d, n_ % 2), ("state", d, (n_ + 1) % 2)
                        u = utmp[d][n_ % 2]
                        uk = ("utmp", d, (n_ % 2) if d == 1 else 0)
                        if d == 0:
                            for hl in range(2):
                                pl = slice(hl * 64, hl * 64 + 64)
                                copy("act", bd[pl, c, hl * 64:(hl + 1) * 64], s_prev[pl, :], [kp], [bdk])
                            tt("dve", u, s_prev, loc, ALU.add, [kp, lk], [uk])
                            ts("dve", s_next, u, et, None, ALU.mult, None, [uk, "ET"], [kn])
                        else:
                            ts("dve", u, s_prev, et, None, ALU.mult, None, [kp, "ET"], [uk])
                            for hl in range(2):
                                pl = slice(hl * 64, hl * 64 + 64)
                                copy("act", bd[pl, c, hl * 64:(hl + 1) * 64], u[pl, :], [uk], [bdk])
                            tt("dve", s_next, u, loc, ALU.add, [uk, lk], [kn])
                for d in range(2):
                    t_ = tiles[d]
                    tsl = slice(t_ * 128, (t_ + 1) * 128)
                    bd = BD[d][bfi]
                    bdk = ("BD", d, bfi)
                    smk = ("SM", d)
                    bo = ps_a()
                    for hl in range(2):
                        mm(psb[bo][:, 0:128], VCz[:, hl, t_, :], SM[d][:, hl, :], hl == 0, False, ["VCz", smk], [("ps", bo)])
                    for c in range(4):
                        mm(psb[bo][:, c * 32:(c + 1) * 32], bd[:, c, :], Qd[d][:, t_ * 128 + c * 32:t_ * 128 + (c + 1) * 32], False, c == 3,
                           [bdk, ("Qd", d)], [("ps", bo)])
                    ofk = ("OF", t_)
                    first_visit = (d == 0 and t_ < 8) or (d == 1 and t_ >= 8)
                    if first_visit:
                        copy("act", OF[:, tsl], psb[bo][:, 0:128], [("ps", bo)], [ofk])
                    else:
                        tt("dve", OF[:, tsl], OF[:, tsl], psb[bo][:, 0:128], ALU.add, [("ps", bo), ofk], [ofk])
            if stop == "C3":
                raise _Stop()
            P.barrier()
            wk[0] = tmark
            RB = walloc(512)
            OB = walloc(512)
            SQ = walloc(256, BF16)
            for tb in range(NB):
                tsl = slice(tb * 512, (tb + 1) * 512)
                act(SQ, OF[:, tsl], AF.Square, [("OF", tb * 4 + i) for i in range(4)], ["SQc"])
                bn = ps_s()
                mm(psb[bn][:], ones64_b, SQ, True, True, ["SQc", "mats"], [("ps", bn)])
                rstd_from_psum(bn, 1.0, 64.0 * EPS, RB, "RBc")
                stt(OB, OF[:, tsl], hn8, RB, ALU.mult, ALU.mult, [("OF", tb * 4 + i) for i in range(4)] + ["RBc", "dv"], ["OBc"])
                tt("dve", cT[:, 6 + jp, tsl], OB, GATE[:, tsl], ALU.mult, ["OBc", "GATE"], ["cT"])

        def out_proj(s, l):
            wreset()
            wo = walloc(4096, BF16, "p (h k c) -> p h k c", h=2, k=8)
            for h in range(2):
                load_w(wo[:, h, :, :].rearrange("p k c -> p (k c)"), ("out", l, h), ("wo", h), "wo%d" % h)
            for tb in range(NB):
                tsl = slice(tb * 512, (tb + 1) * 512)
                for m in range(8):
                    bank = ps_s()
                    h, mc = m // 4, (m % 4) * 128
                    for k in range(KC):
                        mm(psb[bank][:], wo[:, h, k, mc:mc + 128], cT[:, k, tsl], k == 0, k == KC - 1, [("wo", h), "cT"], [("ps", bank)])
                    g1 = modT[:, l, 16 + m, s:s + 1]
                    stt(xT[:, m, tsl], psb[bank][:], g1, xT[:, m, tsl], ALU.mult, ALU.add, [("ps", bank), "modT", "xT"], ["xT"])

        def ffn(s, l):
            wreset()
            tmp = norm_tmp()
            norm_to(s, lambda k: modT[:, l, 32 + k, s:s + 1], lambda k: modT[:, l, 24 + k, s:s + 1],
                    lambda k, tb: hT[:, k, tb * 512:(tb + 1) * 512], lambda k, tb: ["hT"], tmp)
            W1 = [walloc(2048, BF16, "p (k c) -> p k c", k=8) for _ in range(2)]
            W2 = [walloc(2048, BF16, "p (k c) -> p k c", k=32) for _ in range(2)]
            rl = [walloc(512) for _ in range(2)]
            wi = 0
            for tb in range(NB):
                tsl = slice(tb * 512, (tb + 1) * 512)
                for j in range(8):
                    w1 = W1[wi % 2]
                    wk_ = ("W1", wi % 2)
                    load_w(w1.rearrange("p k c -> p (k c)"), ("ff1", l, j), wk_, "w1_%d" % (wi % 2))
                    for i in range(4):
                        bank = ps_s()
                        for k in range(KC):
                            mm(psb[bank][:], w1[:, k, i * 128:(i + 1) * 128], hT[:, k, tsl], k == 0, k == KC - 1, [wk_, "hT"], [("ps", bank)])
                        ri = (j * 4 + i) % 2
                        ts("dve", rl[ri], psb[bank][:], 0.0, None, ALU.max, None, [("ps", bank)], [("rl", ri)])
                        tt("pool", aT[:, j * 4 + i, :], rl[ri], rl[ri], ALU.mult, [("rl", ri)], ["aT"])
                    wi += 1
                for m in range(8):
                    w2 = W2[m % 2]
                    wk_ = ("W2", m % 2)
                    load_w(w2.rearrange("p k c -> p (k c)"), ("ff2", l, m), wk_, "w2_%d" % (m % 2))
                    bank = ps_a()
                    for kk in range(32):
                        mm(psb[bank][:], w2[:, kk, :], aT[:, kk, :], kk == 0, kk == 31, [wk_, "aT"], [("ps", bank)])
                    g2 = modT[:, l, 40 + m, s:s + 1]
                    stt(xT[:, m, tsl], psb[bank][:], g2, xT[:, m, tsl], ALU.mult, ALU.add, [("ps", bank), "modT", "xT"], ["xT"])

        def load_x(s):
            wreset()
            xt = [walloc(1024) for _ in range(2)]
            for t_ in range(NT):
                b = t_ % 2
                dma("sp", xt[b], x_d[s, t_ * 128:(t_ + 1) * 128, :], (), [("xtok", b)], "xin%d" % b)
                for g in range(2):
                    bank = ps_s()
                    for kq in range(4):
                        k = g * 4 + kq
                        tr(psb[bank][:, kq * 128:(kq + 1) * 128], xt[b][:, k * 128:(k + 1) * 128], ident_f, [("xtok", b), "ident_f"], [("ps", bank)])
                    copy(["dve", "act"][g], xT[:, g * 4:(g + 1) * 4, t_ * 128:(t_ + 1) * 128],
                         psb[bank][:].rearrange("p (k t) -> p k t", k=4), [("ps", bank)], ["xT"])

        def final_store(s):
            wreset()
            tmp = norm_tmp()
            yT = walloc(4096, F32, "p (k t) -> p k t", k=8)
            yo = [walloc(1024) for _ in range(2)]
            oi = [0]

            def dst(k, tb):
                return yT[:, k, :]

            sqb, rr, tmpf = tmp
            for tb in range(NB):
                tsl = slice(tb * 512, (tb + 1) * 512)
                bank = ps_s()
                for k in range(KC):
                    sk = ("sq", k % 2)
                    if k % 2 == 0:
                        act(sqb[k % 2], xT[:, k, tsl], AF.Square, ["xT"], [sk])
                    else:
                        tt("pool", sqb[k % 2], xT[:, k, tsl], xT[:, k, tsl], ALU.mult, ["xT"], [sk])
                    mm(psb[bank][:], ones_b, sqb[k % 2], k == 0, k == KC - 1, [sk, "mats"], [("ps", bank)])
                rk = ("nr", tb % 2)
                r = rr[tb % 2]
                rstd_from_psum(bank, 1.0 / D, EPS, r, rk)
                for k in range(KC):
                    tk = ("ntmp", k % 2)
                    tt("dve", tmpf[k % 2], xT[:, k, tsl], r, ALU.mult, ["xT", rk], [tk])
                    act(yT[:, k, :], tmpf[k % 2], AF.Copy, [tk, "vecs"], [("yT", k)], scale=vecs[:, 8 + k:9 + k])
                for ti in range(4):
                    t_ = tb * 4 + ti
                    b = oi[0] % 2
                    oi[0] += 1
                    for g in range(2):
                        bank2 = ps_s()
                        for kq in range(4):
                            k = g * 4 + kq
                            tr(psb[bank2][:, kq * 128:(kq + 1) * 128], yT[:, k, ti * 128:(ti + 1) * 128], ident_f, [("yT", k), "ident_f"], [("ps", bank2)])
                        copy(["dve", "act"][g], yo[b][:, g * 512:(g + 1) * 512], psb[bank2][:], [("ps", bank2)], [("yo", b, g)])
                    dma("sp", out_d[s, t_ * 128:(t_ + 1) * 128, :], yo[b], [("yo", b, 0), ("yo", b, 1)], [("yo", b, 0), ("yo", b, 1), "OUT"], "out%d" % b)

        def main():
            if stop == "startup":
                return
            for s in range(nseq):
                load_x(s)
                if stop == "loadx":
                    return
                for l in range(depth):
                    wreset()
                    tmp = norm_tmp()
                    norm_to(s, lambda k: modT[:, l, 8 + k, s:s + 1], lambda k: modT[:, l, k, s:s + 1],
                            lambda k, tb: hT[:, k, tb * 512:(tb + 1) * 512], lambda k, tb: ["hT"], tmp)
                    if debug and s == 0 and l == 0:
                        [dma("pool", dbg["dbg_h"][:, q * 2048:(q + 1) * 2048], V(HT_O + q * 1024, 1024, BF16), ["hT"], ["dbgo"], "dbgp") for q in range(8)]
                    if stop == "norm1":
                        return
                    mixer_A(s, l)
                    if stop == "A":
                        return
                    for jp in range(3):
                        mixer_B(s, l, jp)
                    if stop == "B":
                        return
                    for jp in range(2):
                        mixer_C(s, l, jp)
                    if debug and s == 0 and l == 0:
                        [dma("pool", dbg["dbg_ct"][:, q * 2048:(q + 1) * 2048], V(CT_O + q * 1024, 1024, BF16), ["cT"], ["dbgo"], "dbgp") for q in range(8)]
                    if stop == "C":
                        return
                    out_proj(s, l)
                    if debug and s == 0 and l == 0:
                        [dma("sp", dbg["dbg_xa"][:, q * 2048:(q + 1) * 2048], V(XT_O + q * 2048, 2048), ["xT"], ["dbgo"], "dbg") for q in range(8)]
                    if stop == "outproj":
                        return
                    ffn(s, l)
                    if stop == "ffn":
                        return
                final_store(s)

        try:
            main()
        except _Stop:
            pass
        P.barrier()
        P.op("sp", None, reads=["OUT", "dbgo"])
        P.emit(nc, st)
    return nc, P


def _prep_inputs(inputs, nseq=4, cores=NCORES):
    mats, bmcol, rope = _host_consts()
    f = lambda a: np.ascontiguousarray(np.asarray(a, dtype=np.float32))
    x = f(inputs["x"])
    c = f(inputs["c"])
    p = np.arange(128)
    vecs = np.zeros((128, 32), np.float32)
    for l in range(2):
        vecs[:, l] = inputs["a_qk_norm"][l, 0][p % 64]
        vecs[:, 2 + l] = inputs["a_qk_norm"][l, 1][p % 64]
        vecs[:, 4 + l] = inputs["diff_subln"][l][p % 64]
        vecs[:, 6 + l] = inputs["hgrn_norm"][l][p % 64]
        for cc in range(2):
            vecs[:, 16 + l * 2 + cc] = inputs["hgrn_lower_bounds"][l][cc * 128 + p]
    for k in range(8):
        vecs[:, 8 + k] = inputs["final_norm"][k * 128 + p]
    dlam = np.ascontiguousarray(np.broadcast_to(f(inputs["diff_lambda"]).reshape(1, 256), (128, 256)))
    b_modT = np.ascontiguousarray(f(inputs["b_mod"]).reshape(2, 48, 128).transpose(2, 0, 1).reshape(128, 96))
    shared = {
        "w_mod": f(inputs["w_mod"]), "b_modT": b_modT, "w_in": f(inputs["w_in"]), "w_out": f(inputs["w_out"]),
        "w_ff1": f(inputs["w_ff1"]), "w_ff2": f(inputs["w_ff2"]), "vecs": vecs, "dlam": dlam, "rope": rope,
        "mats": np.ascontiguousarray(mats.reshape(128, 8 * 128)), "bmcol": bmcol,
    }
    maps = []
    for i in range(cores):
        xs = x[i * nseq:(i + 1) * nseq]
        cs = c[i * nseq:(i + 1) * nseq]
        cT = np.ascontiguousarray(cs.T.reshape(8, 128, nseq).transpose(1, 0, 2).reshape(128, 8 * nseq))
        m = dict(shared)
        m["x"] = np.ascontiguousarray(xs)
        m["cT"] = cT
        maps.append(m)
    return maps


_CACHE = {}


def kernel(**inputs):
    nseq = 4
    if "nc" not in _CACHE:
        _CACHE["nc"] = build(nseq=nseq, depth=2, debug=False)[0]
    nc = _CACHE["nc"]
    maps = _prep_inputs(inputs, nseq=nseq)
    res = run_bass_kernel_spmd(nc, maps, core_ids=list(range(NCORES)))
    out = np.concatenate([np.asarray(r["out"]) for r in res.results], axis=0)
    return out.astype(np.float32)
```
